# Optimizing a Trainium2 kernel written in Bass

```python
import math
import jax, jax.numpy as jnp
from jax import lax
import numpy as np

D_MODEL = 1024
BATCH = 8
SEQ = 4096
DEPTH = 1

EPS = 1e-6
BLOCK_Q = 128
PLE_DIM = 256
MLA_HEADS = 8
MLA_NOPE = 64
MLA_ROPE = 32
MLA_V = 64
MLA_Q_RANK = 384
MLA_KV_RANK = 256
ROPE_THETA = 10000.0
MLA_SCALE = 1.0 / math.sqrt(MLA_NOPE + MLA_ROPE)
SB_HEADS = 8
SB_DIM = 64
SB_SCALE = 1.0 / math.sqrt(SB_DIM)
D_FF = ((8 * D_MODEL // 3 + 255) // 256) * 256
IN_WIDTHS = (MLA_Q_RANK, MLA_KV_RANK, MLA_ROPE,
             SB_HEADS * SB_DIM, SB_HEADS * SB_DIM, SB_HEADS * SB_DIM,
             D_MODEL, D_MODEL)
D_IN = sum(IN_WIDTHS)
NEG_INF = -1e30

kernel_name = "hybrid_mla_stickbreaking_gated_block"


def rms_norm(x, g):
    x32 = x.astype(jnp.float32)
    y = x32 * lax.rsqrt(jnp.mean(x32 * x32, axis=-1, keepdims=True) + EPS)
    return (y * g.astype(jnp.float32)).astype(x.dtype)


def split_cols(t, widths):
    outs, start = [], 0
    for w in widths:
        outs.append(t[..., start:start + w])
        start += w
    return outs


def rope_tables(positions, dtype):
    inv_freq = 1.0 / (ROPE_THETA ** (jnp.arange(0, MLA_ROPE, 2, dtype=jnp.float32) / MLA_ROPE))
    ang = positions.astype(jnp.float32)[..., None] * inv_freq
    return jnp.cos(ang).astype(dtype), jnp.sin(ang).astype(dtype)


def apply_rope(t, cos, sin):
    half = t.shape[-1] // 2
    t1, t2 = t[..., :half], t[..., half:]
    return jnp.concatenate([t1 * cos - t2 * sin, t1 * sin + t2 * cos], axis=-1)


def to_blocks(t):
    B, S, H, d = t.shape
    return t.reshape(B, S // BLOCK_Q, BLOCK_Q, H, d).transpose(1, 0, 3, 2, 4)


def from_blocks(o):
    nb, B, H, Q, d = o.shape
    return o.transpose(1, 0, 3, 2, 4).reshape(B, nb * Q, H * d)


def mla_attention(q_nope, q_pe, k_nope, k_pe, v):
    S = q_nope.shape[1]
    nb = S // BLOCK_Q
    qn, qp = to_blocks(q_nope), to_blocks(q_pe)
    kn = k_nope.transpose(0, 2, 1, 3)
    vv = v.transpose(0, 2, 1, 3)
    kpos = jnp.arange(S)

    def step(args):
        qn_b, qp_b, blk = args
        qpos = blk * BLOCK_Q + jnp.arange(BLOCK_Q)
        s = (jnp.einsum('bhqd,bhkd->bhqk', qn_b, kn)
             + jnp.einsum('bhqr,bkr->bhqk', qp_b, k_pe)).astype(jnp.float32) * MLA_SCALE
        s = jnp.where(kpos[None, :] <= qpos[:, None], s, NEG_INF)
        w = jax.nn.softmax(s, axis=-1).astype(vv.dtype)
        return jnp.einsum('bhqk,bhkd->bhqd', w, vv)

    return from_blocks(lax.map(step, (qn, qp, jnp.arange(nb))))


def stick_breaking_attention(q, k, v):
    S = q.shape[1]
    nb = S // BLOCK_Q
    qb = to_blocks(q)
    kk = k.transpose(0, 2, 1, 3)
    vv = v.transpose(0, 2, 1, 3)
    kpos = jnp.arange(S)

    def step(args):
        q_b, blk = args
        qpos = blk * BLOCK_Q + jnp.arange(BLOCK_Q)
        causal = kpos[None, :] < qpos[:, None]
        z = jnp.einsum('bhqd,bhkd->bhqk', q_b, kk).astype(jnp.float32) * SB_SCALE
        log_1m = jnp.where(causal, jax.nn.log_sigmoid(-z), 0.0)
        after = lax.cumsum(log_1m, axis=3, reverse=True) - log_1m
        a = jnp.where(causal, jnp.exp(jax.nn.log_sigmoid(z) + after), 0.0)
        return jnp.einsum('bhqk,bhkd->bhqd', a.astype(vv.dtype), vv)

    return from_blocks(lax.map(step, (qb, jnp.arange(nb))))


def setup_inputs(seed: int = 0) -> dict:
    key = jax.random.key(seed)
    ks = jax.random.split(key, 24)
    f32 = jnp.float32

    def w(k, shape):
        return jax.random.normal(k, shape, f32) * (shape[-2] ** -0.5)

    def gain(k, shape):
        return 1.0 + 0.01 * jax.random.normal(k, shape, f32)

    L = DEPTH
    return {
        "x": jax.random.normal(ks[0], (BATCH, SEQ, D_MODEL), f32),
        "p": jax.random.normal(ks[1], (DEPTH, BATCH, SEQ, PLE_DIM), f32),
        "positions": jnp.broadcast_to(jnp.arange(SEQ, dtype=jnp.int32), (BATCH, SEQ)),
        "g_mix": gain(ks[2], (L, D_MODEL)),
        "w_in": w(ks[3], (L, D_MODEL, D_IN)),
        "g_q_a": gain(ks[4], (L, MLA_Q_RANK)),
        "w_q_b": w(ks[5], (L, MLA_Q_RANK, MLA_HEADS * (MLA_NOPE + MLA_ROPE))),
        "g_kv_a": gain(ks[6], (L, MLA_KV_RANK)),
        "w_kv_b": w(ks[7], (L, MLA_KV_RANK, MLA_HEADS * (MLA_NOPE + MLA_V))),
        "w_br_mla": w(ks[8], (L, MLA_HEADS * MLA_V, D_MODEL)),
        "w_br_sb": w(ks[9], (L, SB_HEADS * SB_DIM, D_MODEL)),
        "w_out": w(ks[10], (L, D_MODEL, D_MODEL)),
        "g_ffn": gain(ks[11], (L, D_MODEL)),
        "w_ffn_gate": w(ks[12], (L, D_MODEL, D_FF)),
        "w_ffn_up": w(ks[13], (L, D_MODEL, D_FF)),
        "w_ffn_down": w(ks[14], (L, D_FF, D_MODEL)),
        "w_ple_gate": w(ks[15], (L, D_MODEL, D_MODEL)),
        "w_ple_proj": w(ks[16], (L, PLE_DIM, D_MODEL)),
        "g_ple": gain(ks[17], (L, D_MODEL)),
        "g_final": gain(ks[18], (D_MODEL,)),
    }


def reference(x, p, positions, g_mix, w_in, g_q_a, w_q_b, g_kv_a, w_kv_b, w_br_mla, w_br_sb,
              w_out, g_ffn, w_ffn_gate, w_ffn_up, w_ffn_down, w_ple_gate, w_ple_proj, g_ple,
              g_final):
    B, S, _ = x.shape
    cos, sin = rope_tables(positions, x.dtype)
    h = x
    for i in range(DEPTH):
        n = rms_norm(h, g_mix[i])
        proj = n @ w_in[i]
        c_q, c_kv, k_pe, q_sb, k_sb, v_sb, gate_a, gate_b = split_cols(proj, IN_WIDTHS)

        q = (rms_norm(c_q, g_q_a[i]) @ w_q_b[i]).reshape(B, S, MLA_HEADS, MLA_NOPE + MLA_ROPE)
        q_nope, q_pe = q[..., :MLA_NOPE], q[..., MLA_NOPE:]
        q_pe = apply_rope(q_pe, cos[:, :, None, :], sin[:, :, None, :])
        kv = (rms_norm(c_kv, g_kv_a[i]) @ w_kv_b[i]).reshape(B, S, MLA_HEADS, MLA_NOPE + MLA_V)
        k_nope, v_mla = kv[..., :MLA_NOPE], kv[..., MLA_NOPE:]
        k_pe = apply_rope(k_pe, cos, sin)
        o_a = mla_attention(q_nope, q_pe, k_nope, k_pe, v_mla)

        o_b = stick_breaking_attention(q_sb.reshape(B, S, SB_HEADS, SB_DIM),
                                       k_sb.reshape(B, S, SB_HEADS, SB_DIM),
                                       v_sb.reshape(B, S, SB_HEADS, SB_DIM))

        merged = (jax.nn.sigmoid(gate_a) * (o_a @ w_br_mla[i])
                  + jax.nn.sigmoid(gate_b) * (o_b @ w_br_sb[i]))
        h = h + merged @ w_out[i]

        n2 = rms_norm(h, g_ffn[i])
        h = h + (jax.nn.silu(n2 @ w_ffn_gate[i]) * (n2 @ w_ffn_up[i])) @ w_ffn_down[i]

        e = rms_norm(p[i] @ w_ple_proj[i], g_ple[i])
        h = h + jax.nn.sigmoid(h @ w_ple_gate[i]) * e
    return rms_norm(h, g_final)
```

```python
import math
from contextlib import ExitStack

import numpy as np
import concourse.bass as bass
import concourse.mybir as mybir
from concourse.bass_utils import run_bass_kernel_spmd

F32 = mybir.dt.float32
BF16 = mybir.dt.bfloat16
I32 = mybir.dt.int32
AF = mybir.ActivationFunctionType
ALU = mybir.AluOpType

S_LEN = 4096
D = 1024
DFF = 2816
TT = 512
NT = S_LEN // TT
EPS = 1e-6
MLA_SCALE = 1.0 / math.sqrt(96.0)
SB_SCALE = 1.0 / 8.0
TWO_PI = 2.0 * math.pi
C1 = 6.28125
C2 = TWO_PI - C1
PI_C = 3.1415925
SAME_ENGINE_SYNC = True


class Buf:
    __slots__ = ("name", "acc", "w", "r", "dsem", "dcnt")

    def __init__(self, name, acc=False):
        self.name = name
        self.acc = acc
        self.w = {}
        self.r = {}
        self.dsem = None
        self.dcnt = 0


class Rot:
    def __init__(self, items):
        self.items = list(items)
        self.i = 0

    def next(self):
        it = self.items[self.i % len(self.items)]
        self.i += 1
        return it


class Sched:
    def __init__(self, nc, es):
        self.nc = nc
        self.es = es
        self.eng = {"pe": nc.tensor, "act": nc.scalar, "dve": nc.vector,
                    "pool": nc.gpsimd, "sp": nc.sync}
        self.esem = {e: es.enter_context(nc.semaphore("s_" + e))
                     for e in ["pe", "act", "dve", "pool"]}
        self.ecnt = {e: 0 for e in self.esem}
        self.waited = {e: {} for e in self.eng}
        self.dbufs = []
        self.nsem = 0

    def _wait(self, eng, recs):
        need = {}
        for (sem, val, e) in recs:
            if e == eng and (eng == "pe" or not SAME_ENGINE_SYNC):
                continue
            k = id(sem)
            if k not in need or val > need[k][1]:
                need[k] = (sem, val)
        for k, (sem, val) in need.items():
            if self.waited[eng].get(k, 0) >= val:
                continue
            self.eng[eng].wait_ge(sem, val)
            self.waited[eng][k] = val

    @staticmethod
    def _deps(reads, writes):
        recs = []
        for b in reads:
            recs += list(b.w.values())
        for b in writes:
            if not b.acc:
                recs += list(b.w.values())
            recs += list(b.r.values())
        return recs

    def op(self, eng, fn, reads=(), writes=()):
        self._wait(eng, self._deps(reads, writes))
        inst = fn()
        self.ecnt[eng] += 1
        inst.then_inc(self.esem[eng], 1)
        rec = (self.esem[eng], self.ecnt[eng], eng)
        for b in reads:
            b.r[eng] = rec
        for b in writes:
            if b.acc:
                b.w[eng] = rec
            else:
                b.w = {eng: rec}
                b.r = {}
        return inst

    def dma(self, q, out, in_, reads=(), writes=(), owner=None):
        self._wait(q, self._deps(reads, writes))
        if owner.dsem is None:
            owner.dsem = self.es.enter_context(self.nc.semaphore("d%d" % self.nsem))
            self.nsem += 1
            self.dbufs.append(owner)
        inst = self.eng[q].dma_start(out=out, in_=in_)
        owner.dcnt += 16
        inst.then_inc(owner.dsem, 16)
        rec = (owner.dsem, owner.dcnt, "dma")
        key = ("dma", id(owner))
        for b in reads:
            b.r[key] = rec
        for b in writes:
            if b.acc:
                b.w[key] = rec
            else:
                b.w = {key: rec}
                b.r = {}
        return inst

    def barrier(self, skip=()):
        sk = set(id(b) for b in skip)
        recs = [(self.esem[e], self.ecnt[e], "x") for e in self.esem if self.ecnt[e] > 0]
        recs += [(b.dsem, b.dcnt, "dma") for b in self.dbufs if b.dcnt > 0 and id(b) not in sk]
        for e in self.eng:
            self._wait(e, recs)


class Ctx:
    pass


def _sb(stack, nc, name, shape, dt):
    return stack.enter_context(nc.sbuf_tensor("sb_" + name, shape, dt))


def build_program(debug=False, phases="0ABC"):
    nc = bass.Bass("TRN2", target_bir_lowering=False)
    C = Ctx()
    C.nc = nc

    def din(name, shape, dt=F32):
        return nc.dram_tensor(name, shape, dt, kind="ExternalInput").ap()

    def dscr(name, shape, dt):
        return nc.dram_tensor(name, shape, dt, kind="Internal").ap()

    x = din("x", [S_LEN, D])
    p_in = din("p", [S_LEN, 256])
    pos = din("pos", [1, S_LEN], I32)
    WA = din("WA", [D, 2304])
    WG = din("WG", [D, 2048])
    wq2 = din("wq2", [384, 1536])
    wkv = din("wkv", [256, 1024])
    brA = din("brA", [512, D])
    brB = din("brB", [512, D])
    wout = din("wout", [D, D])
    wgate = din("wgate", [D, DFF])
    wup = din("wup", [D, DFF])
    wdown = din("wdown", [DFF, D])
    wpg = din("wpg", [D, D])
    wpp = din("wpp", [256, D])
    gcols_d = din("gcols", [128, 29])
    gfin_d = din("gfin", [1, D])
    consts_d = din("consts", [128, 2])
    out_d = nc.dram_tensor("out", [S_LEN, D], F32, kind="ExternalOutput").ap()

    cos_d = dscr("cos_d", [32, S_LEN], F32)
    sin_d = dscr("sin_d", [32, S_LEN], F32)
    qtm_d = dscr("qtm_d", [8, 96, S_LEN], BF16)
    ktm_d = dscr("ktm_d", [8, 96, S_LEN], BF16)
    vm_d = dscr("vm_d", [S_LEN, 512], BF16)
    qts_d = dscr("qts_d", [512, S_LEN], BF16)
    kts_d = dscr("kts_d", [512, S_LEN], BF16)
    vs_d = dscr("vs_d", [S_LEN, 512], BF16)
    oa_d = dscr("oa_d", [512, S_LEN], BF16)
    ob_d = dscr("ob_d", [512, S_LEN], BF16)
    wsc = {}
    for nm, src in [("WG", WG), ("brA", brA), ("brB", brB), ("wout", wout), ("wgate", wgate),
                    ("wup", wup), ("wdown", wdown), ("wpg", wpg), ("wpp", wpp)]:
        wsc[nm] = (dscr(nm + "_bf", list(src.shape), BF16), src, [])
    dbg = {}
    if debug:
        for nm, shp, dt in [("dbg_qtm", [8, 96, S_LEN], BF16), ("dbg_ktm", [8, 96, S_LEN], BF16),
                            ("dbg_vm", [S_LEN, 512], BF16), ("dbg_qts", [512, S_LEN], BF16),
                            ("dbg_kts", [512, S_LEN], BF16), ("dbg_vs", [S_LEN, 512], BF16),
                            ("dbg_oa", [512, S_LEN], BF16), ("dbg_ob", [512, S_LEN], BF16),
                            ("dbg_cos", [32, S_LEN], F32), ("dbg_sin", [32, S_LEN], F32),
                            ("dbg_h1", [D, TT], F32), ("dbg_h2", [D, TT], F32), ("dbg_h3", [D, TT], F32),
                            ("dbg_mg", [D, TT], BF16)]:
            dbg[nm] = nc.dram_tensor(nm, shp, dt, kind="ExternalOutput").ap()

    B_cos = Buf("cos_d")
    B_sin = Buf("sin_d")
    B_qtm = Buf("qtm_d", acc=True)
    B_ktm = Buf("ktm_d", acc=True)
    B_vm = Buf("vm_d", acc=True)
    B_qts = Buf("qts_d", acc=True)
    B_kts = Buf("kts_d", acc=True)
    B_vs = Buf("vs_d", acc=True)
    B_oa = Buf("oa_d", acc=True)
    B_ob = Buf("ob_d", acc=True)
    B_out = Buf("out_d", acc=True)

    with ExitStack() as es:
        S = Sched(nc, es)
        C.S = S
        pd = [es.enter_context(nc.psum_tensor("pd%d" % i, [128, 1024], F32)) for i in range(4)]
        banks = [(pd[i // 2][:, (i % 2) * 512:(i % 2 + 1) * 512], Buf("pb%d" % i)) for i in range(8)]

        def act(out, in_, func, reads, writes, scale=None, bias=None, accum_out=None):
            kw = {}
            if scale is not None:
                kw["scale"] = scale
            if bias is not None:
                kw["bias"] = bias
            if accum_out is not None:
                kw["accum_out"] = accum_out
            return S.op("act", lambda: nc.scalar.activation(out=out, in_=in_, func=func, **kw), reads, writes)

        def vcopy(eng, out, in_, reads, writes):
            e = {"dve": nc.vector, "pool": nc.gpsimd}[eng]
            return S.op(eng, lambda: e.tensor_copy(out=out, in_=in_), reads, writes)

        def acopy(out, in_, reads, writes):
            return S.op("act", lambda: nc.scalar.copy(out=out, in_=in_), reads, writes)

        def tt(eng, out, in0, in1, op, reads, writes):
            e = {"dve": nc.vector, "pool": nc.gpsimd}[eng]
            return S.op(eng, lambda: e.tensor_tensor(out=out, in0=in0, in1=in1, op=op), reads, writes)

        def ts(eng, out, in0, s1, op0, reads, writes, s2=None, op1=None):
            e = {"dve": nc.vector, "pool": nc.gpsimd}[eng]
            if op1 is None:
                return S.op(eng, lambda: e.tensor_scalar(out=out, in0=in0, scalar1=s1, scalar2=None, op0=op0), reads, writes)
            return S.op(eng, lambda: e.tensor_scalar(out=out, in0=in0, scalar1=s1, scalar2=s2, op0=op0, op1=op1), reads, writes)

        def stt(out, in0, scalar, in1, op0, op1, reads, writes):
            return S.op("dve", lambda: nc.vector.scalar_tensor_tensor(out=out, in0=in0, scalar=scalar, in1=in1, op0=op0, op1=op1), reads, writes)

        def mm(out, lhsT, rhs, start, stop, reads, writes, sgc=False):
            return S.op("pe", lambda: nc.tensor.matmul(out, lhsT=lhsT, rhs=rhs, start=start, stop=stop,
                                                       skip_group_check=sgc), reads, writes)

        def tr(out, in_, ident, reads, writes):
            return S.op("pe", lambda: nc.tensor.transpose(out=out, in_=in_, identity=ident), reads, writes)

        identF = _sb(es, nc, "identF", [128, 128], F32)
        onesF = _sb(es, nc, "onesF", [128, 128], F32)
        onesB = _sb(es, nc, "onesB", [128, 128], BF16)
        tinclB = _sb(es, nc, "tinclB", [128, 128], BF16)
        e0B = _sb(es, nc, "e0B", [128, 128], BF16)
        gcols = _sb(es, nc, "gcols", [128, 29], F32)
        consts = _sb(es, nc, "consts", [128, 2], F32)
        B_const = Buf("const")
        B_gc = Buf("gcols")
        B_cs = Buf("consts")
        S.dma("sp", gcols[:], gcols_d[:, :], writes=[B_gc], owner=B_gc)
        S.dma("sp", consts[:], consts_d[:, :], writes=[B_cs], owner=B_cs)
        S.op("pool", lambda: nc.gpsimd.memset(onesF[:], 1.0), writes=[B_const])
        S.op("pool", lambda: nc.gpsimd.memset(onesB[:], 1.0), writes=[B_const])
        S.op("pool", lambda: nc.gpsimd.affine_select(out=identF[:], in_=onesF[:], pattern=[[1, 128]],
                                                      compare_op=ALU.is_equal, fill=0.0, base=0,
                                                      channel_multiplier=-1), reads=[B_const], writes=[B_const])
        S.op("pool", lambda: nc.gpsimd.affine_select(out=tinclB[:], in_=onesB[:], pattern=[[-1, 128]],
                                                      compare_op=ALU.is_ge, fill=0.0, base=0,
                                                      channel_multiplier=1), reads=[B_const], writes=[B_const])
        S.op("pool", lambda: nc.gpsimd.affine_select(out=e0B[:], in_=onesB[:], pattern=[[1, 128]],
                                                      compare_op=ALU.is_ge, fill=0.0, base=-1,
                                                      channel_multiplier=-1), reads=[B_const], writes=[B_const])
        identB = _sb(es, nc, "identB", [128, 128], BF16)
        S.op("pool", lambda: nc.gpsimd.tensor_copy(out=identB[:], in_=identF[:]), reads=[B_const], writes=[B_const])
        GM, GQ, GKV, GF, GP = 0, 8, 11, 13, 21

        stA = ExitStack()
        WAs = _sb(stA, nc, "WAs", [128, 8, 2304], BF16)
        wq2s = _sb(stA, nc, "wq2s", [128, 3, 1536], BF16)
        wkvs = _sb(stA, nc, "wkvs", [128, 2, 1024], BF16)
        B_WA = [Buf("WA%d" % k) for k in range(8)]
        B_WAh = [Buf("WAh%d" % k) for k in range(8)]
        B_wq2 = [Buf("wq2_%d" % k) for k in range(3)]
        B_wkv = [Buf("wkv_%d" % k) for k in range(2)]
        if "A" in phases:
            for k in range(8):
                S.dma("pool", WAs[:, k, 0:1152], WA[k * 128:(k + 1) * 128, 0:1152], writes=[B_WA[k]], owner=B_WA[k])
                S.dma("pool", WAs[:, k, 1152:2304], WA[k * 128:(k + 1) * 128, 1152:2304], writes=[B_WAh[k]], owner=B_WAh[k])
            for k in range(3):
                S.dma("pool", wq2s[:, k, :], wq2[k * 128:(k + 1) * 128, :], writes=[B_wq2[k]], owner=B_wq2[k])
            for k in range(2):
                S.dma("pool", wkvs[:, k, :], wkv[k * 128:(k + 1) * 128, :], writes=[B_wkv[k]], owner=B_wkv[k])
        if "C" in phases:
            for nm in ["WG", "brA", "brB", "wout", "wgate", "wup", "wdown", "wpg", "wpp"]:
                dst, src, bl = wsc[nm]
                K_, M_ = src.shape
                ncs = (M_ + 2047) // 2048
                cw = M_ // ncs
                for r0 in range(0, K_, 1024):
                    r1 = min(K_, r0 + 1024)
                    for ci in range(ncs):
                        b = Buf("%s_%d_%d" % (nm, r0, ci))
                        S.dma("pool", dst[r0:r1, ci * cw:(ci + 1) * cw], src[r0:r1, ci * cw:(ci + 1) * cw],
                              writes=[b], owner=b)
                        bl.append(b)

        if "0" in phases:
            with ExitStack() as ph:
                posi = _sb(ph, nc, "posi", [128, S_LEN], I32)
                t_a = _sb(ph, nc, "t_a", [128, S_LEN], F32)
                t_b = _sb(ph, nc, "t_b", [128, S_LEN], F32)
                t_c = _sb(ph, nc, "t_c", [128, S_LEN], F32)
                t_d = _sb(ph, nc, "t_d", [128, S_LEN], F32)
                t_e = _sb(ph, nc, "t_e", [128, S_LEN], F32)
                t_f = _sb(ph, nc, "t_f", [128, S_LEN], F32)
                Bp, Ba, Bb, Bc, Bd, Be, Bf = [Buf("ph0_%d" % i) for i in range(7)]
                R = slice(64, 96)
                S.dma("sp", posi[R, :], pos[0:1, :].broadcast_to([32, S_LEN]), writes=[Bp], owner=Bp)
                vcopy("dve", t_a[R, :], posi[R, :], [Bp], [Ba])
                ts("dve", t_b[R, :], t_a[R, :], consts[R, 0:1], ALU.mult, [Ba, B_cs], [Bb])
                ts("dve", t_a[R, :], t_b[R, :], 1.0 / TWO_PI, ALU.mult, [Bb], [Ba])
                vcopy("dve", posi[R, :], t_a[R, :], [Ba], [Bp])
                vcopy("dve", t_a[R, :], posi[R, :], [Bp], [Ba])
                stt(t_c[R, :], t_a[R, :], -C1, t_b[R, :], ALU.mult, ALU.add, [Ba, Bb], [Bc])
                stt(t_c[R, :], t_a[R, :], -C2, t_c[R, :], ALU.mult, ALU.add, [Ba, Bc], [Bc])
                ts("dve", t_d[R, :], t_c[R, :], -PI_C, ALU.max, [Bc], [Bd], s2=PI_C, op1=ALU.min)
                act(t_e[R, :], t_d[R, :], AF.Sin, [Bd], [Be])
                ts("dve", t_e[R, :], t_e[R, :], consts[R, 1:2], ALU.mult, [Be, B_cs], [Be])
                ts("dve", t_d[R, :], t_c[R, :], math.pi / 2, ALU.add, [Bc], [Bd])
                ts("dve", t_a[R, :], t_d[R, :], math.pi, ALU.is_gt, [Bd], [Ba])
                stt(t_d[R, :], t_a[R, :], -TWO_PI, t_d[R, :], ALU.mult, ALU.add, [Ba, Bd], [Bd])
                ts("dve", t_d[R, :], t_d[R, :], -PI_C, ALU.max, [Bd], [Bd], s2=PI_C, op1=ALU.min)
                act(t_f[R, :], t_d[R, :], AF.Sin, [Bd], [Bf])
                S.dma("sp", cos_d[:, :], t_f[R, :], reads=[Bf], writes=[B_cos], owner=Bf)
                S.dma("sp", sin_d[:, :], t_e[R, :], reads=[Be], writes=[B_sin], owner=Be)
                bg = B_WA + B_WAh + B_wq2 + B_wkv
                for nm_ in wsc:
                    bg = bg + wsc[nm_][2]
                S.barrier(skip=bg)

        if "A" in phases:
            with ExitStack() as ph:
                xtok = _sb(ph, nc, "xtok", [128, 4, D], F32)
                xT = _sb(ph, nc, "xT", [128, 8, TT], F32)
                sqc = [_sb(ph, nc, "sqc%d" % i, [128, TT], F32) for i in range(3)]
                nT2 = [_sb(ph, nc, "nT_%d" % i, [128, 8, TT], BF16) for i in range(2)]
                lnv = _sb(ph, nc, "lnv", [128, TT], F32)
                rstd = _sb(ph, nc, "rstd", [128, TT], F32)
                rstdq = _sb(ph, nc, "rstdq", [128, TT], F32)
                rstdkv = _sb(ph, nc, "rstdkv", [128, TT], F32)
                cT = _sb(ph, nc, "cT", [128, 5, TT], F32)
                cn = _sb(ph, nc, "cn", [128, 5, TT], BF16)
                cst = _sb(ph, nc, "cst", [128, 2, TT], F32)
                tm1 = [_sb(ph, nc, "tm1_%d" % i, [128, TT], F32) for i in range(2)]
                tm2 = [_sb(ph, nc, "tm2_%d" % i, [128, TT], F32) for i in range(2)]
                kpr = _sb(ph, nc, "kpr", [128, TT], BF16)
                QTm = _sb(ph, nc, "QTm", [128, 8, TT], BF16)
                KTm = _sb(ph, nc, "KTm", [128, 8, TT], BF16)
                Vm = _sb(ph, nc, "Vm", [128, 4, 512], BF16)
                QTs = _sb(ph, nc, "QTs", [128, 4, TT], BF16)
                KTs = _sb(ph, nc, "KTs", [128, 4, TT], BF16)
                Vs = _sb(ph, nc, "Vs", [128, 4, 512], BF16)
                B_xtok = Buf("xtok")
                B_xT = [Buf("xT%d" % i) for i in range(8)]
                B_sqc = [Buf("sqc%d" % i) for i in range(3)]
                B_nT2 = [[Buf("nT%d_%d" % (j, i)) for i in range(8)] for j in range(2)]
                B_lnv, B_rstd, B_rstdq, B_rstdkv = Buf("lnv"), Buf("rstd"), Buf("rstdq"), Buf("rstdkv")
                B_cT = [Buf("cT%d" % i) for i in range(5)]
                B_cn = [Buf("cn%d" % i) for i in range(5)]
                B_cst = Buf("cst")
                B_cst1 = Buf("cst1")
                B_tm1 = [Buf("tm1_%d" % i) for i in range(2)]
                B_tm2 = [Buf("tm2_%d" % i) for i in range(2)]
                B_kpr = Buf("kpr")
                B_QTm = [Buf("QTm%d" % i) for i in range(8)]
                B_QTmr = [Buf("QTmr%d" % i) for i in range(8)]
                B_KTm = [Buf("KTm%d" % i) for i in range(8)]
                B_KTmr = [Buf("KTmr%d" % i) for i in range(8)]
                B_Vm = [Buf("Vm%d" % i) for i in range(4)]
                B_QTs = [Buf("QTs%d" % i) for i in range(4)]
                B_KTs = [Buf("KTs%d" % i) for i in range(4)]
                B_Vs = [Buf("Vs%d" % i) for i in range(4)]
                O_QTm, O_KTm, O_Vm, O_QTs, O_KTs, O_Vs = [Buf("own%d" % i) for i in range(6)]
                bk = Rot(banks)
                sqr = Rot(list(zip(sqc, B_sqc)))
                tmr = Rot(list(zip(tm1, B_tm1, tm2, B_tm2)))
                alt = [0]

                def evac_copy(out, in_, reads, writes):
                    alt[0] += 1
                    if alt[0] % 2:
                        acopy(out, in_, reads, writes)
                    else:
                        vcopy("dve", out, in_, reads, writes)

                def stats(srcs, nfeat, dst, dstb):
                    bt, bb = bk.next()
                    for i, (ap, b) in enumerate(srcs):
                        sq, sqb = sqr.next()
                        act(sq[:, :], ap, AF.Square, [b], [sqb])
                        mm(bt[:, :], onesF[:, :], sq[:, :], i == 0, i == len(srcs) - 1, [sqb, B_const], [bb])
                    act(lnv[:, :], bt[:, :], AF.Ln, [bb], [B_lnv], scale=1.0 / nfeat, bias=EPS)
                    act(dst[:, :], lnv[:, :], AF.Exp, [B_lnv], [dstb], scale=-0.5)

                RR = slice(64, 96)

                def front(t):
                    tsl = slice(t * TT, (t + 1) * TT)
                    nT, B_nT = nT2[t % 2], B_nT2[t % 2]
                    S.dma("sp", xtok[:], x[tsl, :].rearrange("(j p) d -> p j d", p=128), writes=[B_xtok], owner=B_xtok)
                    for c in range(8):
                        bt, bb = bk.next()
                        for j in range(4):
                            tr(bt[:, j * 128:(j + 1) * 128], xtok[:, j, c * 128:(c + 1) * 128], identF[:, :],
                               [B_xtok, B_const], [bb])
                        evac_copy(xT[:, c, :], bt[:, :], [bb], [B_xT[c]])
                    stats([(xT[:, c, :], B_xT[c]) for c in range(8)], float(D), rstd, B_rstd)
                    for c in range(8):
                        stt(nT[:, c, :], xT[:, c, :], gcols[:, GM + c:GM + c + 1], rstd[:, :], ALU.mult, ALU.mult,
                            [B_xT[c], B_gc, B_rstd], [B_nT[c]])
                def mid(t):
                    tsl = slice(t * TT, (t + 1) * TT)
                    nT, B_nT = nT2[t % 2], B_nT2[t % 2]
                    S.dma("sp", cst[64:96, 0, :], cos_d[:, tsl], reads=[B_cos], writes=[B_cst], owner=B_cst)
                    S.dma("sp", cst[64:96, 1, :], sin_d[:, tsl], reads=[B_sin], writes=[B_cst1], owner=B_cst1)

                    def wa_b(col0, M):
                        r = []
                        if col0 < 1152:
                            r += B_WA
                        if col0 + M > 1152:
                            r += B_WAh
                        return r

                    def proj_chunk(col0, M):
                        bt, bb = bk.next()
                        wb = wa_b(col0, M)
                        for kc in range(8):
                            mm(bt[0:M, :], WAs[:, kc, col0:col0 + M], nT[:, kc, :], kc == 0, kc == 7,
                               [B_nT[kc]] + wb, [bb])
                        return bt, bb

                    for i in range(5):
                        bt, bb = proj_chunk(i * 128, 128)
                        evac_copy(cT[:, i, :], bt[:, :], [bb], [B_cT[i]])
                    for i in range(4):
                        bt, bb = proj_chunk(672 + i * 128, 128)
                        evac_copy(QTs[:, i, :], bt[:, :], [bb], [B_QTs[i]])
                    for i in range(4):
                        bt, bb = proj_chunk(1184 + i * 128, 128)
                        evac_copy(KTs[:, i, :], bt[:, :], [bb], [B_KTs[i]])
                    btp, bbp = proj_chunk(576, 96)
                    bts, bbs = proj_chunk(2208, 96)
                    t1, b1, t2, b2 = tmr.next()
                    tt("dve", t1[RR, :], btp[RR, :], cst[RR, 0, :], ALU.mult, [bbp, B_cst], [b1])
                    tt("dve", t2[RR, :], bts[RR, :], cst[RR, 1, :], ALU.mult, [bbs, B_cst1], [b2])
                    tt("pool", kpr[RR, :], t1[RR, :], t2[RR, :], ALU.add, [b1, b2], [B_kpr])
                    for j in range(4):
                        bt, bb = bk.next()
                        for kc in range(8):
                            mm(bt[:, :], nT[:, kc, j * 128:(j + 1) * 128], WAs[:, kc, 1696:2208], kc == 0, kc == 7,
                               [B_nT[kc]] + B_WAh, [bb])
                        evac_copy(Vs[:, j, :], bt[:, :], [bb], [B_Vs[j]])
                def back(t):
                    tsl = slice(t * TT, (t + 1) * TT)
                    stats([(cT[:, i, :], B_cT[i]) for i in range(3)], 384.0, rstdq, B_rstdq)
                    stats([(cT[:, i, :], B_cT[i]) for i in range(3, 5)], 256.0, rstdkv, B_rstdkv)
                    for i in range(5):
                        rs_, rb_ = (rstdq, B_rstdq) if i < 3 else (rstdkv, B_rstdkv)
                        gc = (GQ + i) if i < 3 else (GKV + i - 3)
                        stt(cn[:, i, :], cT[:, i, :], gcols[:, gc:gc + 1], rs_[:, :], ALU.mult, ALU.mult,
                            [B_cT[i], B_gc, rb_], [B_cn[i]])
                    for h in range(8):
                        bta, bba = bk.next()
                        btb, bbb = bk.next()
                        for kc in range(3):
                            mm(bta[0:96, :], wq2s[:, kc, h * 96:(h + 1) * 96], cn[:, kc, :], kc == 0, kc == 2,
                               [B_cn[kc], B_wq2[kc]], [bba])
                        for kc in range(3):
                            mm(btb[0:96, :], wq2s[:, kc, 768 + h * 96:768 + (h + 1) * 96], cn[:, kc, :], kc == 0, kc == 2,
                               [B_cn[kc], B_wq2[kc]], [bbb])
                        acopy(QTm[0:64, h, :], bta[0:64, :], [bba], [B_QTm[h]])
                        t1, b1, t2, b2 = tmr.next()
                        tt("dve", t1[RR, :], bta[RR, :], cst[RR, 0, :], ALU.mult, [bba, B_cst], [b1])
                        tt("dve", t2[RR, :], btb[RR, :], cst[RR, 1, :], ALU.mult, [bbb, B_cst1], [b2])
                        tt("pool", QTm[RR, h, :], t1[RR, :], t2[RR, :], ALU.add, [b1, b2], [B_QTmr[h]])
                    for h in range(8):
                        bt, bb = bk.next()
                        for kc in range(2):
                            mm(bt[0:64, :], wkvs[:, kc, h * 128:h * 128 + 64], cn[:, 3 + kc, :], kc == 0, kc == 1,
                               [B_cn[3 + kc], B_wkv[kc]], [bb])
                        evac_copy(KTm[0:64, h, :], bt[0:64, :], [bb], [B_KTm[h]])
                        vcopy("pool", KTm[RR, h, :], kpr[RR, :], [B_kpr], [B_KTmr[h]])
                    for j in range(4):
                        bt, bb = bk.next()
                        for kc in range(2):
                            mm(bt[:, :].rearrange("p (h c) -> p h c", c=64), cn[:, 3 + kc, j * 128:(j + 1) * 128],
                               wkvs[:, kc, :].rearrange("p (h c) -> p h c", c=128)[:, :, 64:128], kc == 0, kc == 1,
                               [B_cn[3 + kc], B_wkv[kc]], [bb])
                        evac_copy(Vm[:, j, :], bt[:, :], [bb], [B_Vm[j]])
                    S.dma("sp", qtm_d[:, :, tsl].rearrange("h d t -> d h t"), QTm[0:96, :, :],
                          reads=B_QTm + B_QTmr, writes=[B_qtm], owner=O_QTm)
                    S.dma("sp", ktm_d[:, :, tsl].rearrange("h d t -> d h t"), KTm[0:96, :, :],
                          reads=B_KTm + B_KTmr, writes=[B_ktm], owner=O_KTm)
                    S.dma("sp", vm_d[tsl, :].rearrange("(j p) c -> p j c", p=128), Vm[:],
                          reads=B_Vm, writes=[B_vm], owner=O_Vm)
                    S.dma("sp", qts_d[:, tsl].rearrange("(c p) t -> p c t", p=128), QTs[:],
                          reads=B_QTs, writes=[B_qts], owner=O_QTs)
                    S.dma("sp", kts_d[:, tsl].rearrange("(c p) t -> p c t", p=128), KTs[:],
                          reads=B_KTs, writes=[B_kts], owner=O_KTs)
                    S.dma("sp", vs_d[tsl, :].rearrange("(j p) c -> p j c", p=128), Vs[:],
                          reads=B_Vs, writes=[B_vs], owner=O_Vs)

                front(0)
                for t in range(NT):
                    mid(t)
                    if t + 1 < NT:
                        front(t + 1)
                    back(t)
                S.barrier()
        stA.close()

        C.__dict__.update(locals())
        if "B" in phases:
            phase_B(C)
        if "C" in phases:
            phase_C(C)

        if debug:
            Bd_ = Buf("dbgown")
            for nm, src in [("dbg_qtm", qtm_d), ("dbg_ktm", ktm_d), ("dbg_vm", vm_d), ("dbg_qts", qts_d),
                            ("dbg_kts", kts_d), ("dbg_vs", vs_d), ("dbg_oa", oa_d), ("dbg_ob", ob_d),
                            ("dbg_cos", cos_d), ("dbg_sin", sin_d)]:
                S.barrier()
                S.dma("sp", dbg[nm], src, owner=Bd_)
        S.barrier()
    return nc


def phase_B(C):
    nc, S, banks = C.nc, C.S, C.banks
    act, vcopy, acopy, tt, ts, stt, mm = C.act, C.vcopy, C.acopy, C.tt, C.ts, C.stt, C.mm
    onesB, tinclB, B_const, e0B = C.onesB, C.tinclB, C.B_const, C.e0B
    with ExitStack() as phm:
        negB = _sb(phm, nc, "negB", [128, 512], BF16)
        maskM = [_sb(phm, nc, "maskM%d" % j, [128, 512], BF16) for j in range(4)]
        maskS = [_sb(phm, nc, "maskS%d" % j, [128, 512], BF16) for j in range(4)]
        B_mask = Buf("masks")
        S.op("pool", lambda: nc.gpsimd.memset(negB[:], -10000.0), writes=[B_mask])
        for j in range(4):
            S.op("pool", lambda: nc.gpsimd.affine_select(out=maskM[j][:], in_=negB[:], pattern=[[-1, 512]],
                                                          compare_op=ALU.is_ge, fill=0.0, base=j * 128 - 1,
                                                          channel_multiplier=1), reads=[B_mask], writes=[B_mask])
            S.op("pool", lambda: nc.gpsimd.affine_select(out=maskS[j][:], in_=negB[:], pattern=[[-1, 512]],
                                                          compare_op=ALU.is_ge, fill=0.0, base=j * 128,
                                                          channel_multiplier=1), reads=[B_mask], writes=[B_mask])
        phase_B_inner(C, maskM, maskS, B_mask)


def phase_B_inner(C, maskM, maskS, B_mask):
    nc, S, banks = C.nc, C.S, C.banks
    act, vcopy, acopy, tt, ts, stt, mm = C.act, C.vcopy, C.acopy, C.tt, C.ts, C.stt, C.mm
    onesB, tinclB, B_const, e0B = C.onesB, C.tinclB, C.B_const, C.e0B
    with ExitStack() as ph:
        QT = [_sb(ph, nc, "mQT%d" % i, [128, S_LEN], BF16) for i in range(2)]
        KT = [_sb(ph, nc, "mKT%d" % i, [128, S_LEN], BF16) for i in range(2)]
        VV = [_sb(ph, nc, "mV%d" % i, [128, 32, 128], BF16) for i in range(2)]
        B_QT = [Buf("mQT%d" % i) for i in range(2)]
        B_KT = [Buf("mKT%d" % i) for i in range(2)]
        B_VV = [Buf("mV%d" % i) for i in range(2)]
        B_V1 = [Buf("mV1_%d" % i) for i in range(2)]
        Pt = [_sb(ph, nc, "Pt%d" % i, [128, 512], BF16) for i in range(4)]
        B_Pt = [Buf("Pt%d" % i) for i in range(4)]
        rec = [_sb(ph, nc, "rec%d" % i, [128, 512], F32) for i in range(2)]
        B_rec = [Buf("rec%d" % i) for i in range(2)]
        ob = [_sb(ph, nc, "ob%d" % i, [128, 512], BF16) for i in range(2)]
        B_ob_ = [Buf("ob%d" % i) for i in range(2)]
        for i in range(2):
            S.op("pool", lambda: nc.gpsimd.memset(VV[i][:, :, 64:128], 1.0), writes=[B_V1[i]])

        def load_mla(h):
            i = h % 2
            S.dma("sp", QT[i][0:96, :], C.qtm_d[h, :, :], reads=[C.B_qtm], writes=[B_QT[i]], owner=B_QT[i])
            S.dma("sp", KT[i][0:96, :], C.ktm_d[h, :, :], reads=[C.B_ktm], writes=[B_KT[i]], owner=B_KT[i])
            S.dma("sp", VV[i][:, :, 0:64], C.vm_d[:, h * 64:(h + 1) * 64].rearrange("(k p) c -> p k c", p=128),
                  reads=[C.B_vm], writes=[B_VV[i]], owner=B_VV[i])

        Pt2 = [_sb(ph, nc, "Pt2_%d" % i, [128, 1024], BF16) for i in range(3)]
        B_Pt2 = [Buf("Pt2_%d" % i) for i in range(3)]
        spr = Rot([0, 1])
        ptr = Rot(list(zip(Pt2, B_Pt2)))
        fin = Rot(list(zip(rec, B_rec, ob, B_ob_)))
        pd = C.pd
        load_mla(0)
        for h in range(8):
            if h + 1 < 8:
                load_mla(h + 1)
            i = h % 2
            units = [(qi, kb) for qi in range(8) for kb in range(0, 4 * qi + 4, 2)]
            n = len(units)
            st = {}

            def s1(s):
                qi, kb0 = units[s]
                pi = spr.next()
                for sl in range(2):
                    kb = kb0 + sl
                    bt, bb = banks[2 * pi + sl]
                    diag = kb >= 4 * qi
                    mm(bt[:, :], KT[i][0:96, kb * 128:(kb + 1) * 128], QT[i][0:96, qi * 512:(qi + 1) * 512], True, not diag,
                       [B_KT[i], B_QT[i]], [bb])
                    if diag:
                        mm(bt[:, :], C.identB[:, :], maskM[kb - 4 * qi][:, :], False, True, [B_const, B_mask], [bb])
                pt, pb = ptr.next()
                act(pt[:, :], pd[pi][:, :], AF.Exp, [banks[2 * pi][1], banks[2 * pi + 1][1]], [pb], scale=MLA_SCALE)
                st[s] = (pt, pb)

            def s2(s):
                qi, kb0 = units[s]
                pt, pb = st.pop(s)
                ot, obuf = banks[4 + (qi % 2)]
                last = 4 * qi + 3
                for sl in range(2):
                    kb = kb0 + sl
                    mm(ot[:, :], VV[i][:, kb, :], pt[:, sl * 512:(sl + 1) * 512], kb == 0, kb == last,
                       [pb, B_VV[i], B_V1[i]], [obuf])
                if kb0 + 1 == last:
                    rc, rb, o_, o_b = fin.next()
                    S.op("dve", lambda: nc.vector.reciprocal(out=rc[0:64, :], in_=ot[64:128, :]), [obuf], [rb])
                    tt("dve", o_[0:64, :], ot[0:64, :], rc[0:64, :], ALU.mult, [obuf, rb], [o_b])
                    S.dma("sp", C.oa_d[h * 64:(h + 1) * 64, qi * 512:(qi + 1) * 512], o_[0:64, :],
                          reads=[o_b], writes=[C.B_oa], owner=o_b)

            SK = 1
            for step in range(n + SK):
                if step < n:
                    s1(step)
                if step - SK >= 0:
                    s2(step - SK)
        S.barrier()

    with ExitStack() as ph:
        QT = [_sb(ph, nc, "sQT%d" % i, [128, S_LEN], BF16) for i in range(2)]
        KT = [_sb(ph, nc, "sKT%d" % i, [128, S_LEN], BF16) for i in range(2)]
        VV = [_sb(ph, nc, "sV%d" % i, [128, 32, 128], BF16) for i in range(2)]
        B_QT = [Buf("sQT%d" % i) for i in range(2)]
        B_KT = [Buf("sKT%d" % i) for i in range(2)]
        B_VV = [Buf("sV%d" % i) for i in range(2)]
        B_pad = Buf("sPad")
        for i in range(2):
            S.op("pool", lambda: nc.gpsimd.memset(QT[i][64:128, :], 0.0), writes=[B_pad])
            S.op("pool", lambda: nc.gpsimd.memset(KT[i][64:128, :], 0.0), writes=[B_pad])
            S.op("pool", lambda: nc.gpsimd.memset(VV[i][:, :, 64:128], 0.0), writes=[B_pad])
        ee = [_sb(ph, nc, "ee%d" % i, [128, 1024], F32) for i in range(3)]
        B_ee = [Buf("ee%d" % i) for i in range(3)]
        ln = [_sb(ph, nc, "ln%d" % i, [128, 1024], BF16) for i in range(4)]
        B_ln = [Buf("ln%d" % i) for i in range(4)]
        te = [_sb(ph, nc, "te%d" % i, [128, 1024], F32) for i in range(2)]
        B_te = [Buf("te%d" % i) for i in range(2)]
        aa = [_sb(ph, nc, "aa%d" % i, [128, 1024], BF16) for i in range(3)]
        B_aa = [Buf("aa%d" % i) for i in range(3)]
        ob = [_sb(ph, nc, "sob%d" % i, [128, 512], BF16) for i in range(2)]
        B_ob_ = [Buf("sob%d" % i) for i in range(2)]

        def load_sb(h):
            i = h % 2
            S.dma("sp", QT[i][0:64, :], C.qts_d[h * 64:(h + 1) * 64, :], reads=[C.B_qts], writes=[B_QT[i]], owner=B_QT[i])
            S.dma("sp", KT[i][0:64, :], C.kts_d[h * 64:(h + 1) * 64, :], reads=[C.B_kts], writes=[B_KT[i]], owner=B_KT[i])
            S.dma("sp", VV[i][:, :, 0:64], C.vs_d[:, h * 64:(h + 1) * 64].rearrange("(k p) c -> p k c", p=128),
                  reads=[C.B_vs], writes=[B_VV[i]], owner=B_VV[i])

        zpr = Rot([0, 3])
        eer = Rot(list(zip(ee, B_ee)))
        lnr = Rot(list(zip(ln, B_ln)))
        ter = Rot(list(zip(te, B_te)))
        aar = Rot(list(zip(aa, B_aa)))
        obr = Rot(list(zip(ob, B_ob_)))
        pd = C.pd
        load_sb(0)
        for h in range(8):
            if h + 1 < 8:
                load_sb(h + 1)
            i = h % 2
            units = []
            for m in range(4):
                ta = [(2 * m, kb) for kb in range(8 * m + 3, -1, -1)]
                tb_ = [(2 * m + 1, kb) for kb in range(8 * m + 7, -1, -1)]
                for idx in range(len(tb_)):
                    units.append((ta[idx] if idx < len(ta) else None, tb_[idx]))
            n = len(units)
            st = {}

            def tl(u):
                return [(sl, t) for sl, t in enumerate(u) if t is not None]

            def s1(s):
                u = units[s]
                zi = zpr.next()
                for sl, (qi, kb) in tl(u):
                    bt, bb = banks[2 * zi + sl]
                    diag = kb >= 4 * qi
                    mm(bt[:, :], KT[i][:, kb * 128:(kb + 1) * 128], QT[i][:, qi * 512:(qi + 1) * 512], True, not diag,
                       [B_KT[i], B_QT[i], B_pad], [bb])
                    if diag:
                        mm(bt[:, :], C.identB[:, :], maskS[kb - 4 * qi][:, :], False, True, [B_const, B_mask], [bb])
                lo = 0 if u[0] is not None else 512
                zb = [banks[2 * zi + sl][1] for sl, _ in tl(u)]
                e_, eb = eer.next()
                act(e_[:, lo:1024], pd[zi][:, lo:1024], AF.Exp, zb, [eb], scale=SB_SCALE)
                l_, lb = lnr.next()
                act(l_[:, lo:1024], e_[:, lo:1024], AF.Ln, [eb], [lb], bias=1.0)
                st[s] = (e_, eb, l_, lb, lo)

            def s2a(s):
                e_, eb, l_, lb, lo = st[s][:5]
                for sl, (qi, kb) in tl(units[s]):
                    ct, cb = banks[2 + sl]
                    mm(ct[:, :], tinclB[:, :], l_[:, sl * 512:(sl + 1) * 512], kb == 4 * qi + 3, kb == 0,
                       [lb, B_const], [cb], sgc=True)

            def s2b(s):
                e_, eb, l_, lb, lo = st[s][:5]
                cbs = [banks[2 + sl][1] for sl, _ in tl(units[s])]
                t_, tb = ter.next()
                act(t_[:, lo:1024], pd[1][:, lo:1024], AF.Exp, cbs, [tb], scale=-1.0)
                a_, ab = aar.next()
                tt("dve", a_[:, lo:1024], e_[:, lo:1024], t_[:, lo:1024], ALU.mult, [eb, tb], [ab])
                st[s] = (e_, eb, l_, lb, lo, a_, ab)

            def s2u(s):
                e_, eb, l_, lb, lo, a_, ab = st[s]
                for sl, (qi, kb) in tl(units[s]):
                    if kb > 0:
                        ct, cb = banks[2 + sl]
                        mm(ct[:, :], e0B[:, :], l_[:, sl * 512:(sl + 1) * 512], False, False, [lb, B_const], [cb], sgc=True)

            def s3(s):
                e_, eb, l_, lb, lo, a_, ab = st.pop(s)
                for sl, (qi, kb) in tl(units[s]):
                    ot, obuf = banks[4 + sl]
                    mm(ot[:, :], VV[i][:, kb, :], a_[:, sl * 512:(sl + 1) * 512], kb == 4 * qi + 3, kb == 0,
                       [ab, B_VV[i], B_pad], [obuf])
                    if kb == 0:
                        o_, o_b = obr.next()
                        vcopy("dve", o_[0:64, :], ot[0:64, :], [obuf], [o_b])
                        S.dma("sp", C.ob_d[h * 64:(h + 1) * 64, qi * 512:(qi + 1) * 512], o_[0:64, :],
                              reads=[o_b], writes=[C.B_ob], owner=o_b)

            for step in range(n + 2):
                if step < n:
                    s1(step)
                if step - 2 >= 0:
                    s2u(step - 2)
                if 0 <= step - 1 < n:
                    s2a(step - 1)
                    s2b(step - 1)
                if step - 2 >= 0:
                    s3(step - 2)
        S.barrier()


def phase_C(C):
    nc, S, banks = C.nc, C.S, C.banks
    act, vcopy, acopy, tt, ts, stt, mm, tr = C.act, C.vcopy, C.acopy, C.tt, C.ts, C.stt, C.mm, C.tr
    onesF, identF, B_const, gcols, B_gc = C.onesF, C.identF, C.B_const, C.gcols, C.B_gc
    GM, GF, GP = C.GM, C.GF, C.GP
    wsc = C.wsc
    with ExitStack() as ph:
        xtok = _sb(ph, nc, "c_xtok", [128, 4, D], F32)
        hT2 = [_sb(ph, nc, "c_hT_%d" % i, [128, 8, TT], F32) for i in range(2)]
        nX = _sb(ph, nc, "c_nX", [128, 8, TT], BF16)
        rstd_x = _sb(ph, nc, "c_rstd_x", [128, TT], F32)
        rstd_e = _sb(ph, nc, "c_rstd_e", [128, TT], F32)
        rstd_o = _sb(ph, nc, "c_rstd_o", [128, TT], F32)
        nT = _sb(ph, nc, "c_nT", [128, 8, TT], BF16)
        lnv = _sb(ph, nc, "c_lnv", [128, TT], F32)
        rstd = _sb(ph, nc, "c_rstd", [128, TT], F32)
        oT = _sb(ph, nc, "c_oT", [128, 8, TT], BF16)
        sgab = _sb(ph, nc, "c_sgab", [128, 8, TT], F32)
        sgv = sgab[:, :, :].bitcast(BF16)
        mf = _sb(ph, nc, "c_mf", [128, 8, TT], F32)
        actT = _sb(ph, nc, "c_actT", [128, 22, TT], BF16)
        mg = actT
        ptok = _sb(ph, nc, "c_ptok", [128, 4, 256], F32)
        pT = _sb(ph, nc, "c_pT", [128, 2, TT], BF16)
        tmpa = [_sb(ph, nc, "c_tmpa%d" % i, [128, TT], F32) for i in range(2)]
        tmpb = [_sb(ph, nc, "c_tmpb%d" % i, [128, TT], F32) for i in range(2)]
        wbuf = [_sb(ph, nc, "c_w%d" % i, [128, 8, 512], BF16) for i in range(4)]
        gfin = _sb(ph, nc, "c_gfin", [128, D], F32)
        otok = [_sb(ph, nc, "c_otok%d" % i, [128, D], F32) for i in range(3)]
        ssq = [_sb(ph, nc, "c_ssq%d" % i, [128, TT], F32) for i in range(3)]
        B_ssq = [Buf("c_ssq%d" % i) for i in range(3)]
        ssr = Rot(list(zip(ssq, B_ssq)))
        rcol = _sb(ph, nc, "c_rcol", [128, 4], F32)
        B_rcol = Buf("c_rcol")
        B_xtok = Buf("c_xtok")
        B_hT2 = [[Buf("c_hT%d_%d" % (j, i)) for i in range(8)] for j in range(2)]
        B_nX = [Buf("c_nX%d" % i) for i in range(8)]
        B_rstd_x, B_rstd_e, B_rstd_o = Buf("c_rstd_x"), Buf("c_rstd_e"), Buf("c_rstd_o")
        B_sq8 = [Buf("c_sq8_%d" % i) for i in range(8)]
        B_nT = [Buf("c_nT%d" % i) for i in range(8)]
        B_lnv, B_rstd = Buf("c_lnv"), Buf("c_rstd")
        B_oTa, B_oTb = Buf("c_oTa"), Buf("c_oTb")
        B_sga = [Buf("c_sga%d" % i) for i in range(8)]
        B_sgb = [Buf("c_sgb%d" % i) for i in range(8)]
        B_mf = [Buf("c_mf%d" % i) for i in range(8)]
        B_actT = [Buf("c_actT%d" % i) for i in range(22)]
        B_mg = B_actT[0:8]
        B_ptok = Buf("c_ptok")
        B_pT = [Buf("c_pT%d" % i) for i in range(2)]
        B_tmpa = [Buf("c_tmpa%d" % i) for i in range(2)]
        B_tmpb = [Buf("c_tmpb%d" % i) for i in range(2)]
        B_w = [Buf("c_w%d" % i) for i in range(4)]
        B_gfin = Buf("c_gfin")
        B_otok = [Buf("c_otok%d" % i) for i in range(3)]
        B_ssum = Buf("c_ssum")
        bk = Rot(banks)
        wrot = Rot(list(zip(wbuf, B_w)))
        tar = Rot(list(zip(tmpa, B_tmpa)))
        tbr = Rot(list(zip(tmpb, B_tmpb)))
        otr = Rot(list(zip(otok, B_otok)))
        alt = [0]
        S.dma("sp", gfin[:], C.gfin_d[0:1, :].broadcast_to([128, D]), writes=[B_gfin], owner=B_gfin)

        def evac_copy(out, in_, reads, writes):
            alt[0] += 1
            if alt[0] % 2:
                acopy(out, in_, reads, writes)
            else:
                vcopy("dve", out, in_, reads, writes)

        def stats_pre(srcs):
            sqs = []
            for i, (ap, b) in enumerate(srcs):
                sq, sqb = sgab[:, i, :], [B_sga[i], B_sgb[i]]
                act(sq, ap, AF.Square, [b], sqb)
                sqs.append((sq, sqb))
            lvl = sqs
            k = 0
            while len(lvl) > 2:
                nxt = []
                for j in range(0, len(lvl), 2):
                    (a_, ab_), (b_, bb_) = lvl[j], lvl[j + 1]
                    eng = "pool" if k % 2 == 0 else "dve"
                    k += 1
                    tt(eng, a_, a_, b_, ALU.add, ab_ + bb_, ab_)
                    nxt.append((a_, ab_))
                lvl = nxt
            res, resb = ssr.next()
            (a_, ab_), (b_, bb_) = lvl
            tt("dve", res[:, :], a_, b_, ALU.add, ab_ + bb_, [resb])
            return res, resb

        def stats_post(pre, nfeat, dst, dstb):
            res, resb = pre
            bt, bb = bk.next()
            mm(bt[:, :], onesF[:, :], res[:, :], True, True, [resb, B_const], [bb])
            act(lnv[:, :], bt[:, :], AF.Ln, [bb], [B_lnv], scale=1.0 / nfeat, bias=EPS)
            act(dst[:, :], lnv[:, :], AF.Exp, [B_lnv], [dstb], scale=-0.5)

        def stats(srcs, nfeat, dst, dstb):
            stats_post(stats_pre(srcs), nfeat, dst, dstb)

        def linear(wname, K, col0, ncols_total, rhs_fn, rhs_bufs, evac_fn, cgs=None):
            Wd, _, wdeps = wsc[wname]
            KC = K // 128
            for cg0 in range(0, ncols_total, 512):
                if cgs is not None and (cg0 // 512) not in cgs:
                    continue
                ncol = min(512, ncols_total - cg0)
                noc = ncol // 128
                bks = [bk.next() for _ in range(noc)]
                for kg0 in range(0, KC, 8):
                    nk = min(8, KC - kg0)
                    wt, wb = wrot.next()
                    S.dma("sp", wt[:, 0:nk, 0:ncol],
                          Wd[kg0 * 128:(kg0 + nk) * 128, col0 + cg0:col0 + cg0 + ncol].rearrange("(k p) c -> p k c", p=128),
                          reads=wdeps, writes=[wb], owner=wb)
                    for oc in range(noc):
                        bt, bb = bks[oc]
                        for k in range(nk):
                            kc = kg0 + k
                            mm(bt[:, :], wt[:, k, oc * 128:(oc + 1) * 128], rhs_fn(kc), kc == 0, kc == KC - 1,
                               [wb, rhs_bufs[kc]], [bb])
                for oc in range(noc):
                    evac_fn(cg0 // 128 + oc, bks[oc])

        def front(t):
            tsl = slice(t * TT, (t + 1) * TT)
            hT, B_hT = hT2[t % 2], B_hT2[t % 2]
            S.dma("sp", xtok[:], C.x[tsl, :].rearrange("(j p) d -> p j d", p=128), writes=[B_xtok], owner=B_xtok)
            S.dma("sp", oT[:, 0:4, :], C.oa_d[:, tsl].rearrange("(c p) t -> p c t", p=128), reads=[C.B_oa],
                  writes=[B_oTa], owner=B_oTa)
            S.dma("sp", oT[:, 4:8, :], C.ob_d[:, tsl].rearrange("(c p) t -> p c t", p=128), reads=[C.B_ob],
                  writes=[B_oTb], owner=B_oTb)
            S.dma("sp", ptok[:], C.p_in[tsl, :].rearrange("(j p) d -> p j d", p=128), writes=[B_ptok], owner=B_ptok)
            for c in range(8):
                bt, bb = bk.next()
                for j in range(4):
                    tr(bt[:, j * 128:(j + 1) * 128], xtok[:, j, c * 128:(c + 1) * 128], identF[:, :],
                       [B_xtok, B_const], [bb])
                evac_copy(hT[:, c, :], bt[:, :], [bb], [B_hT[c]])
            return stats_pre([(hT[:, c, :], B_hT[c]) for c in range(8)])

        def front_b(t, pre):
            hT, B_hT = hT2[t % 2], B_hT2[t % 2]
            stats_post(pre, float(D), rstd_x, B_rstd_x)
            for c in range(8):
                stt(nX[:, c, :], hT[:, c, :], gcols[:, GM + c:GM + c + 1], rstd_x[:, :], ALU.mult, ALU.mult,
                    [B_hT[c], B_gc, B_rstd_x], [B_nX[c]])

        def gates(t, parts=(0, 1, 2, 3)):
            for part in parts:
                if part < 2:
                    linear("WG", D, 0, D, lambda kc: nX[:, kc, :], B_nX,
                           lambda oc, b: act(sgv[:, oc, 0:TT], b[0][:, :], AF.Sigmoid, [b[1]], [B_sga[oc]]),
                           cgs=[part])
                else:
                    linear("WG", D, D, D, lambda kc: nX[:, kc, :], B_nX,
                           lambda oc, b: act(sgv[:, oc, TT:2 * TT], b[0][:, :], AF.Sigmoid, [b[1]], [B_sgb[oc]]),
                           cgs=[part - 2])

        def mix(t):
            hT, B_hT = hT2[t % 2], B_hT2[t % 2]
            linear("brA", 512, 0, D, lambda kc: oT[:, kc, :], [B_oTa] * 4,
                   lambda oc, b: tt("dve", mf[:, oc, :], b[0][:, :], sgv[:, oc, 0:TT], ALU.mult, [b[1], B_sga[oc]], [B_mf[oc]]))

            def ev_brB(oc, b):
                ta, tab = tar.next()
                tt("dve", ta[:, :], b[0][:, :], sgv[:, oc, TT:2 * TT], ALU.mult, [b[1], B_sgb[oc]], [tab])
                tt("pool", mg[:, oc, :], ta[:, :], mf[:, oc, :], ALU.add, [tab, B_mf[oc]], [B_mg[oc]])
            linear("brB", 512, 0, D, lambda kc: oT[:, 4 + kc, :], [B_oTb] * 4, ev_brB)
            linear("wout", D, 0, D, lambda kc: mg[:, kc, :], B_mg,
                   lambda oc, b: tt("dve", hT[:, oc, :], b[0][:, :], hT[:, oc, :], ALU.add, [b[1], B_hT[oc]], [B_hT[oc]]))
            if C.debug and t == 0:
                S.dma("sp", C.dbg["dbg_h1"].rearrange("(c p) t -> p c t", p=128), hT[:], reads=B_hT, owner=Buf("dh1"))
                S.dma("sp", C.dbg["dbg_mg"].rearrange("(c p) t -> p c t", p=128), mg[:, 0:8, :], reads=B_mg, owner=Buf("dmg"))

        def ple(t):
            for c in range(2):
                bt, bb = bk.next()
                for j in range(4):
                    tr(bt[:, j * 128:(j + 1) * 128], ptok[:, j, c * 128:(c + 1) * 128], identF[:, :],
                       [B_ptok, B_const], [bb])
                evac_copy(pT[:, c, :], bt[:, :], [bb], [B_pT[c]])
            linear("wpp", 256, 0, D, lambda kc: pT[:, kc, :], B_pT,
                   lambda oc, b: acopy(mf[:, oc, :], b[0][:, :], [b[1]], [B_mf[oc]]))
            return stats_pre([(mf[:, c, :], B_mf[c]) for c in range(8)])

        def h_pre(t):
            hT, B_hT = hT2[t % 2], B_hT2[t % 2]
            return stats_pre([(hT[:, c, :], B_hT[c]) for c in range(8)])

        def ffn(t, pre):
            hT, B_hT = hT2[t % 2], B_hT2[t % 2]
            stats_post(pre, float(D), rstd, B_rstd)
            for c in range(8):
                stt(nT[:, c, :], hT[:, c, :], gcols[:, GF + c:GF + c + 1], rstd[:, :], ALU.mult, ALU.mult,
                    [B_hT[c], B_gc, B_rstd], [B_nT[c]])
            linear("wgate", D, 0, DFF, lambda kc: nT[:, kc, :], B_nT,
                   lambda oc, b: act(actT[:, oc, :], b[0][:, :], AF.Silu, [b[1]], [B_actT[oc]]))
            linear("wup", D, 0, DFF, lambda kc: nT[:, kc, :], B_nT,
                   lambda oc, b: tt("dve", actT[:, oc, :], b[0][:, :], actT[:, oc, :], ALU.mult, [b[1], B_actT[oc]], [B_actT[oc]]))
            linear("wdown", DFF, 0, D, lambda kc: actT[:, kc, :], B_actT,
                   lambda oc, b: tt("dve", hT[:, oc, :], b[0][:, :], hT[:, oc, :], ALU.add, [b[1], B_hT[oc]], [B_hT[oc]]))
            if C.debug and t == 0:
                S.dma("sp", C.dbg["dbg_h2"].rearrange("(c p) t -> p c t", p=128), hT[:], reads=B_hT, owner=Buf("dh2"))

        def pgate(t):
            hT, B_hT = hT2[t % 2], B_hT2[t % 2]
            for c in range(8):
                vcopy("pool", nT[:, c, :], hT[:, c, :], [B_hT[c]], [B_nT[c]])

            def ev_pg(oc, b):
                ta, tab = tar.next()
                e1, e2 = ("dve", "pool") if oc % 2 == 0 else ("pool", "dve")
                act(ta[:, :], b[0][:, :], AF.Sigmoid, [b[1]], [tab])
                tt(e1, ta[:, :], ta[:, :], mf[:, oc, :], ALU.mult, [tab, B_mf[oc]], [tab])
                tt(e2, hT[:, oc, :], hT[:, oc, :], ta[:, :], ALU.add, [B_hT[oc], tab], [B_hT[oc]])
            linear("wpg", D, 0, D, lambda kc: nT[:, kc, :], B_nT, ev_pg)
            if C.debug and t == 0:
                S.dma("sp", C.dbg["dbg_h3"].rearrange("(c p) t -> p c t", p=128), hT[:], reads=B_hT, owner=Buf("dh3"))

        def final_pre(t, pre):
            stats_post(pre, float(D), rstd_o, B_rstd_o)
            bt, bb = bk.next()
            for j in range(4):
                tr(bt[:, j * 128:(j + 1) * 128], rstd_o[:, j * 128:(j + 1) * 128], identF[:, :],
                   [B_rstd_o, B_const], [bb])
            for j in range(4):
                vcopy("dve", rcol[:, j:j + 1], bt[:, j * 128:j * 128 + 1], [bb], [B_rcol])

        def final_j(t, j):
            hT, B_hT = hT2[t % 2], B_hT2[t % 2]
            ot_, otb = otr.next()
            for half in range(2):
                bt, bb = bk.next()
                for c4 in range(4):
                    c = half * 4 + c4
                    tr(bt[:, c4 * 128:(c4 + 1) * 128], hT[:, c, j * 128:(j + 1) * 128], identF[:, :],
                       [B_hT[c], B_const], [bb])
                stt(ot_[:, half * 512:(half + 1) * 512], bt[:, :], rcol[:, j:j + 1], gfin[:, half * 512:(half + 1) * 512],
                    ALU.mult, ALU.mult, [bb, B_rcol, B_gfin], [otb])
            r0 = t * TT + j * 128
            S.dma("sp", C.out_d[r0:r0 + 128, :], ot_[:, :], reads=[otb], writes=[C.B_out], owner=otb)

        front_b(0, front(0))
        gates(0)
        for t in range(NT):
            mix(t)
            p1 = h_pre(t)
            p2 = ple(t)
            p3 = front(t + 1) if t + 1 < NT else None
            ffn_norm_pre = p1
            stats_post(p2, float(D), rstd_e, B_rstd_e)
            for c in range(8):
                stt(mf[:, c, :], mf[:, c, :], gcols[:, GP + c:GP + c + 1], rstd_e[:, :], ALU.mult, ALU.mult,
                    [B_mf[c], B_gc, B_rstd_e], [B_mf[c]])
            ffn(t, ffn_norm_pre)
            if p3 is not None:
                front_b(t + 1, p3)
            pgate(t)
            p4 = h_pre(t)
            nxt = t + 1 < NT
            if nxt:
                gates(t + 1, (0, 1))
            final_pre(t, p4)
            final_j(t, 0)
            if nxt:
                gates(t + 1, (2,))
            final_j(t, 1)
            if nxt:
                gates(t + 1, (3,))
            final_j(t, 2)
            final_j(t, 3)
        S.barrier()


def prep_inputs(inputs):
    f = lambda k: np.ascontiguousarray(np.asarray(inputs[k]))
    w_in = f("w_in")[0]
    kswap = np.concatenate([w_in[:, 576:640], w_in[:, 656:672], w_in[:, 640:656]], axis=1)
    WA = np.ascontiguousarray(np.concatenate([w_in[:, 0:2208], kswap], axis=1))
    WG = np.ascontiguousarray(w_in[:, 2208:4256])
    wq = f("w_q_b")[0].reshape(384, 8, 96)
    wq_swap = np.concatenate([wq[:, :, 0:64], wq[:, :, 80:96], wq[:, :, 64:80]], axis=2)
    wq2 = np.ascontiguousarray(np.concatenate([wq.reshape(384, 768), wq_swap.reshape(384, 768)], axis=1))
    col = lambda g: g.reshape(-1, 128).T
    gcols = np.ascontiguousarray(np.concatenate(
        [col(f("g_mix")[0]), col(f("g_q_a")[0]), col(f("g_kv_a")[0]), col(f("g_ffn")[0]), col(f("g_ple")[0])],
        axis=1).astype(np.float32))
    consts = np.zeros((128, 2), np.float32)
    inv_freq = (1.0 / (10000.0 ** (np.arange(0, 32, 2, dtype=np.float32) / 32.0))).astype(np.float32)
    for j in range(32):
        consts[64 + j, 0] = inv_freq[j % 16]
        consts[64 + j, 1] = -1.0 if j < 16 else 1.0
    shared = {
        "WA": WA, "WG": WG, "wq2": wq2, "wkv": f("w_kv_b")[0],
        "brA": f("w_br_mla")[0], "brB": f("w_br_sb")[0], "wout": f("w_out")[0],
        "wgate": f("w_ffn_gate")[0], "wup": f("w_ffn_up")[0], "wdown": f("w_ffn_down")[0],
        "wpg": f("w_ple_gate")[0], "wpp": f("w_ple_proj")[0],
        "gcols": gcols, "gfin": f("g_final").reshape(1, D), "consts": consts,
    }
    x = f("x")
    p = f("p")[0]
    pos = f("positions").astype(np.int32)
    maps = []
    for b in range(x.shape[0]):
        m = dict(shared)
        m["x"] = np.ascontiguousarray(x[b])
        m["p"] = np.ascontiguousarray(p[b])
        m["pos"] = np.ascontiguousarray(pos[b].reshape(1, S_LEN))
        maps.append(m)
    return maps


def kernel(**inputs):
    maps = prep_inputs(inputs)
    nc = build_program()
    res = run_bass_kernel_spmd(nc, maps, core_ids=list(range(len(maps))))
    return np.stack([np.asarray(r["out"]) for r in res.results], axis=0).astype(np.float32)
```

```python
import math
from contextlib import ExitStack

import numpy as np
import concourse.bass as bass
import concourse.mybir as mybir
from concourse.bass_utils import run_bass_kernel_spmd

F32 = mybir.dt.float32
BF16 = mybir.dt.bfloat16
I32 = mybir.dt.int32
AF = mybir.ActivationFunctionType
ALU = mybir.AluOpType

S_LEN = 4096
D = 1024
DFF = 2816
TT = 512
NT = S_LEN // TT
EPS = 1e-6
MLA_SCALE = 1.0 / math.sqrt(96.0)
SB_SCALE = 1.0 / 8.0
TWO_PI = 2.0 * math.pi
C1 = 6.28125
C2 = TWO_PI - C1
PI_C = 3.1415925
SAME_ENGINE_SYNC = True


class Buf:
    __slots__ = ("name", "acc", "w", "r", "dsem", "dcnt")

    def __init__(self, name, acc=False):
        self.name = name
        self.acc = acc
        self.w = {}
        self.r = {}
        self.dsem = None
        self.dcnt = 0


class Rot:
    def __init__(self, items):
        self.items = list(items)
        self.i = 0

    def next(self):
        it = self.items[self.i % len(self.items)]
        self.i += 1
        return it


class Sched:
    def __init__(self, nc, es):
        self.nc = nc
        self.es = es
        self.eng = {"pe": nc.tensor, "act": nc.scalar, "dve": nc.vector,
                    "pool": nc.gpsimd, "sp": nc.sync}
        self.esem = {e: es.enter_context(nc.semaphore("s_" + e))
                     for e in ["pe", "act", "dve", "pool"]}
        self.ecnt = {e: 0 for e in self.esem}
        self.waited = {e: {} for e in self.eng}
        self.dbufs = []
        self.nsem = 0

    def _wait(self, eng, recs):
        need = {}
        for (sem, val, e) in recs:
            if e == eng and (eng == "pe" or not SAME_ENGINE_SYNC):
                continue
            k = id(sem)
            if k not in need or val > need[k][1]:
                need[k] = (sem, val)
        for k, (sem, val) in need.items():
            if self.waited[eng].get(k, 0) >= val:
                continue
            self.eng[eng].wait_ge(sem, val)
            self.waited[eng][k] = val

    @staticmethod
    def _deps(reads, writes):
        recs = []
        for b in reads:
            recs += list(b.w.values())
        for b in writes:
            if not b.acc:
                recs += list(b.w.values())
            recs += list(b.r.values())
        return recs

    def op(self, eng, fn, reads=(), writes=()):
        self._wait(eng, self._deps(reads, writes))
        inst = fn()
        self.ecnt[eng] += 1
        inst.then_inc(self.esem[eng], 1)
        rec = (self.esem[eng], self.ecnt[eng], eng)
        for b in reads:
            b.r[eng] = rec
        for b in writes:
            if b.acc:
                b.w[eng] = rec
            else:
                b.w = {eng: rec}
                b.r = {}
        return inst

    def dma(self, q, out, in_, reads=(), writes=(), owner=None):
        self._wait(q, self._deps(reads, writes))
        if owner.dsem is None:
            owner.dsem = self.es.enter_context(self.nc.semaphore("d%d" % self.nsem))
            self.nsem += 1
            self.dbufs.append(owner)
        inst = self.eng[q].dma_start(out=out, in_=in_)
        owner.dcnt += 16
        inst.then_inc(owner.dsem, 16)
        rec = (owner.dsem, owner.dcnt, "dma")
        key = ("dma", id(owner))
        for b in reads:
            b.r[key] = rec
        for b in writes:
            if b.acc:
                b.w[key] = rec
            else:
                b.w = {key: rec}
                b.r = {}
        return inst

    def barrier(self, skip=()):
        sk = set(id(b) for b in skip)
        recs = [(self.esem[e], self.ecnt[e], "x") for e in self.esem if self.ecnt[e] > 0]
        recs += [(b.dsem, b.dcnt, "dma") for b in self.dbufs if b.dcnt > 0 and id(b) not in sk]
        for e in self.eng:
            self._wait(e, recs)


class Ctx:
    pass


def _sb(stack, nc, name, shape, dt):
    return stack.enter_context(nc.sbuf_tensor("sb_" + name, shape, dt))


def build_program(debug=False, phases="0ABC"):
    nc = bass.Bass("TRN2", target_bir_lowering=False)
    C = Ctx()
    C.nc = nc

    def din(name, shape, dt=F32):
        return nc.dram_tensor(name, shape, dt, kind="ExternalInput").ap()

    def dscr(name, shape, dt):
        return nc.dram_tensor(name, shape, dt, kind="Internal").ap()

    x = din("x", [S_LEN, D])
    p_in = din("p", [S_LEN, 256])
    pos = din("pos", [1, S_LEN], I32)
    WA = din("WA", [D, 2304])
    WG = din("WG", [D, 2048])
    wq2 = din("wq2", [384, 1536])
    wkv = din("wkv", [256, 1024])
    brA = din("brA", [512, D])
    brB = din("brB", [512, D])
    wout = din("wout", [D, D])
    wgate = din("wgate", [D, DFF])
    wup = din("wup", [D, DFF])
    wdown = din("wdown", [DFF, D])
    wpg = din("wpg", [D, D])
    wpp = din("wpp", [256, D])
    gcols_d = din("gcols", [128, 29])
    gfin_d = din("gfin", [1, D])
    consts_d = din("consts", [128, 2])
    out_d = nc.dram_tensor("out", [S_LEN, D], F32, kind="ExternalOutput").ap()

    cos_d = dscr("cos_d", [32, S_LEN], F32)
    sin_d = dscr("sin_d", [32, S_LEN], F32)
    qtm_d = dscr("qtm_d", [8, 96, S_LEN], BF16)
    ktm_d = dscr("ktm_d", [8, 96, S_LEN], BF16)
    vm_d = dscr("vm_d", [S_LEN, 512], BF16)
    qts_d = dscr("qts_d", [512, S_LEN], BF16)
    kts_d = dscr("kts_d", [512, S_LEN], BF16)
    vs_d = dscr("vs_d", [S_LEN, 512], BF16)
    oa_d = dscr("oa_d", [512, S_LEN], BF16)
    ob_d = dscr("ob_d", [512, S_LEN], BF16)
    wsc = {}
    for nm, src in [("WG", WG), ("brA", brA), ("brB", brB), ("wout", wout), ("wgate", wgate),
                    ("wup", wup), ("wdown", wdown), ("wpg", wpg), ("wpp", wpp)]:
        wsc[nm] = (dscr(nm + "_bf", list(src.shape), BF16), src, [])
    dbg = {}
    if debug:
        for nm, shp, dt in [("dbg_qtm", [8, 96, S_LEN], BF16), ("dbg_ktm", [8, 96, S_LEN], BF16),
                            ("dbg_vm", [S_LEN, 512], BF16), ("dbg_qts", [512, S_LEN], BF16),
                            ("dbg_kts", [512, S_LEN], BF16), ("dbg_vs", [S_LEN, 512], BF16),
                            ("dbg_oa", [512, S_LEN], BF16), ("dbg_ob", [512, S_LEN], BF16),
                            ("dbg_cos", [32, S_LEN], F32), ("dbg_sin", [32, S_LEN], F32),
                            ("dbg_h1", [D, TT], F32), ("dbg_h2", [D, TT], F32), ("dbg_h3", [D, TT], F32),
                            ("dbg_mg", [D, TT], BF16)]:
            dbg[nm] = nc.dram_tensor(nm, shp, dt, kind="ExternalOutput").ap()

    B_cos = Buf("cos_d")
    B_sin = Buf("sin_d")
    B_qtm = Buf("qtm_d", acc=True)
    B_ktm = Buf("ktm_d", acc=True)
    B_vm = Buf("vm_d", acc=True)
    B_qts = Buf("qts_d", acc=True)
    B_kts = Buf("kts_d", acc=True)
    B_vs = Buf("vs_d", acc=True)
    B_oa = Buf("oa_d", acc=True)
    B_ob = Buf("ob_d", acc=True)
    B_out = Buf("out_d", acc=True)

    with ExitStack() as es:
        S = Sched(nc, es)
        C.S = S
        pd = [es.enter_context(nc.psum_tensor("pd%d" % i, [128, 1024], F32)) for i in range(4)]
        banks = [(pd[i // 2][:, (i % 2) * 512:(i % 2 + 1) * 512], Buf("pb%d" % i)) for i in range(8)]

        def act(out, in_, func, reads, writes, scale=None, bias=None, accum_out=None):
            kw = {}
            if scale is not None:
                kw["scale"] = scale
            if bias is not None:
                kw["bias"] = bias
            if accum_out is not None:
                kw["accum_out"] = accum_out
            return S.op("act", lambda: nc.scalar.activation(out=out, in_=in_, func=func, **kw), reads, writes)

        def vcopy(eng, out, in_, reads, writes):
            e = {"dve": nc.vector, "pool": nc.gpsimd}[eng]
            return S.op(eng, lambda: e.tensor_copy(out=out, in_=in_), reads, writes)

        def acopy(out, in_, reads, writes):
            return S.op("act", lambda: nc.scalar.copy(out=out, in_=in_), reads, writes)

        def tt(eng, out, in0, in1, op, reads, writes):
            e = {"dve": nc.vector, "pool": nc.gpsimd}[eng]
            return S.op(eng, lambda: e.tensor_tensor(out=out, in0=in0, in1=in1, op=op), reads, writes)

        def ts(eng, out, in0, s1, op0, reads, writes, s2=None, op1=None):
            e = {"dve": nc.vector, "pool": nc.gpsimd}[eng]
            if op1 is None:
                return S.op(eng, lambda: e.tensor_scalar(out=out, in0=in0, scalar1=s1, scalar2=None, op0=op0), reads, writes)
            return S.op(eng, lambda: e.tensor_scalar(out=out, in0=in0, scalar1=s1, scalar2=s2, op0=op0, op1=op1), reads, writes)

        def stt(out, in0, scalar, in1, op0, op1, reads, writes):
            return S.op("dve", lambda: nc.vector.scalar_tensor_tensor(out=out, in0=in0, scalar=scalar, in1=in1, op0=op0, op1=op1), reads, writes)

        def mm(out, lhsT, rhs, start, stop, reads, writes, sgc=False):
            return S.op("pe", lambda: nc.tensor.matmul(out, lhsT=lhsT, rhs=rhs, start=start, stop=stop,
                                                       skip_group_check=sgc), reads, writes)

        def tr(out, in_, ident, reads, writes):
            return S.op("pe", lambda: nc.tensor.transpose(out=out, in_=in_, identity=ident), reads, writes)

        identF = _sb(es, nc, "identF", [128, 128], F32)
        onesF = _sb(es, nc, "onesF", [128, 128], F32)
        onesB = _sb(es, nc, "onesB", [128, 128], BF16)
        tinclB = _sb(es, nc, "tinclB", [128, 128], BF16)
        e0B = _sb(es, nc, "e0B", [128, 128], BF16)
        gcols = _sb(es, nc, "gcols", [128, 29], F32)
        consts = _sb(es, nc, "consts", [128, 2], F32)
        B_const = Buf("const")
        B_gc = Buf("gcols")
        B_cs = Buf("consts")
        S.dma("sp", gcols[:], gcols_d[:, :], writes=[B_gc], owner=B_gc)
        S.dma("sp", consts[:], consts_d[:, :], writes=[B_cs], owner=B_cs)
        S.op("pool", lambda: nc.gpsimd.memset(onesF[:], 1.0), writes=[B_const])
        S.op("pool", lambda: nc.gpsimd.memset(onesB[:], 1.0), writes=[B_const])
        S.op("pool", lambda: nc.gpsimd.affine_select(out=identF[:], in_=onesF[:], pattern=[[1, 128]],
                                                      compare_op=ALU.is_equal, fill=0.0, base=0,
                                                      channel_multiplier=-1), reads=[B_const], writes=[B_const])
        S.op("pool", lambda: nc.gpsimd.affine_select(out=tinclB[:], in_=onesB[:], pattern=[[-1, 128]],
                                                      compare_op=ALU.is_ge, fill=0.0, base=0,
                                                      channel_multiplier=1), reads=[B_const], writes=[B_const])
        S.op("pool", lambda: nc.gpsimd.affine_select(out=e0B[:], in_=onesB[:], pattern=[[1, 128]],
                                                      compare_op=ALU.is_ge, fill=0.0, base=-1,
                                                      channel_multiplier=-1), reads=[B_const], writes=[B_const])
        identB = _sb(es, nc, "identB", [128, 128], BF16)
        S.op("pool", lambda: nc.gpsimd.tensor_copy(out=identB[:], in_=identF[:]), reads=[B_const], writes=[B_const])
        GM, GQ, GKV, GF, GP = 0, 8, 11, 13, 21

        stA = ExitStack()
        WAs = _sb(stA, nc, "WAs", [128, 8, 2304], BF16)
        wq2s = _sb(stA, nc, "wq2s", [128, 3, 1536], BF16)
        wkvs = _sb(stA, nc, "wkvs", [128, 2, 1024], BF16)
        B_WA = [Buf("WA%d" % k) for k in range(8)]
        B_WAh = [Buf("WAh%d" % k) for k in range(8)]
        B_wq2 = [Buf("wq2_%d" % k) for k in range(3)]
        B_wkv = [Buf("wkv_%d" % k) for k in range(2)]
        if "A" in phases:
            for k in range(8):
                S.dma("pool", WAs[:, k, 0:1152], WA[k * 128:(k + 1) * 128, 0:1152], writes=[B_WA[k]], owner=B_WA[k])
                S.dma("pool", WAs[:, k, 1152:2304], WA[k * 128:(k + 1) * 128, 1152:2304], writes=[B_WAh[k]], owner=B_WAh[k])
            for k in range(3):
                S.dma("pool", wq2s[:, k, :], wq2[k * 128:(k + 1) * 128, :], writes=[B_wq2[k]], owner=B_wq2[k])
            for k in range(2):
                S.dma("pool", wkvs[:, k, :], wkv[k * 128:(k + 1) * 128, :], writes=[B_wkv[k]], owner=B_wkv[k])
        if "C" in phases:
            for nm in ["WG", "brA", "brB", "wout", "wgate", "wup", "wdown", "wpg", "wpp"]:
                dst, src, bl = wsc[nm]
                K_, M_ = src.shape
                ncs = (M_ + 2047) // 2048
                cw = M_ // ncs
                for r0 in range(0, K_, 1024):
                    r1 = min(K_, r0 + 1024)
                    for ci in range(ncs):
                        b = Buf("%s_%d_%d" % (nm, r0, ci))
                        S.dma("pool", dst[r0:r1, ci * cw:(ci + 1) * cw], src[r0:r1, ci * cw:(ci + 1) * cw],
                              writes=[b], owner=b)
                        bl.append(b)

        if "0" in phases:
            with ExitStack() as ph:
                posi = _sb(ph, nc, "posi", [128, S_LEN], I32)
                t_a = _sb(ph, nc, "t_a", [128, S_LEN], F32)
                t_b = _sb(ph, nc, "t_b", [128, S_LEN], F32)
                t_c = _sb(ph, nc, "t_c", [128, S_LEN], F32)
                t_d = _sb(ph, nc, "t_d", [128, S_LEN], F32)
                t_e = _sb(ph, nc, "t_e", [128, S_LEN], F32)
                t_f = _sb(ph, nc, "t_f", [128, S_LEN], F32)
                Bp, Ba, Bb, Bc, Bd, Be, Bf = [Buf("ph0_%d" % i) for i in range(7)]
                R = slice(64, 96)
                S.dma("sp", posi[R, :], pos[0:1, :].broadcast_to([32, S_LEN]), writes=[Bp], owner=Bp)
                vcopy("dve", t_a[R, :], posi[R, :], [Bp], [Ba])
                ts("dve", t_b[R, :], t_a[R, :], consts[R, 0:1], ALU.mult, [Ba, B_cs], [Bb])
                ts("dve", t_a[R, :], t_b[R, :], 1.0 / TWO_PI, ALU.mult, [Bb], [Ba])
                vcopy("dve", posi[R, :], t_a[R, :], [Ba], [Bp])
                vcopy("dve", t_a[R, :], posi[R, :], [Bp], [Ba])
                stt(t_c[R, :], t_a[R, :], -C1, t_b[R, :], ALU.mult, ALU.add, [Ba, Bb], [Bc])
                stt(t_c[R, :], t_a[R, :], -C2, t_c[R, :], ALU.mult, ALU.add, [Ba, Bc], [Bc])
                ts("dve", t_d[R, :], t_c[R, :], -PI_C, ALU.max, [Bc], [Bd], s2=PI_C, op1=ALU.min)
                act(t_e[R, :], t_d[R, :], AF.Sin, [Bd], [Be])
                ts("dve", t_e[R, :], t_e[R, :], consts[R, 1:2], ALU.mult, [Be, B_cs], [Be])
                ts("dve", t_d[R, :], t_c[R, :], math.pi / 2, ALU.add, [Bc], [Bd])
                ts("dve", t_a[R, :], t_d[R, :], math.pi, ALU.is_gt, [Bd], [Ba])
                stt(t_d[R, :], t_a[R, :], -TWO_PI, t_d[R, :], ALU.mult, ALU.add, [Ba, Bd], [Bd])
                ts("dve", t_d[R, :], t_d[R, :], -PI_C, ALU.max, [Bd], [Bd], s2=PI_C, op1=ALU.min)
                act(t_f[R, :], t_d[R, :], AF.Sin, [Bd], [Bf])
                S.dma("sp", cos_d[:, :], t_f[R, :], reads=[Bf], writes=[B_cos], owner=Bf)
                S.dma("sp", sin_d[:, :], t_e[R, :], reads=[Be], writes=[B_sin], owner=Be)
                bg = B_WA + B_WAh + B_wq2 + B_wkv
                for nm_ in wsc:
                    bg = bg + wsc[nm_][2]
                S.barrier(skip=bg)

        if "A" in phases:
            with ExitStack() as ph:
                xtok = _sb(ph, nc, "xtok", [128, 4, D], F32)
                xT = _sb(ph, nc, "xT", [128, 8, TT], F32)
                sqc = [_sb(ph, nc, "sqc%d" % i, [128, TT], F32) for i in range(3)]
                nT2 = [_sb(ph, nc, "nT_%d" % i, [128, 8, TT], BF16) for i in range(2)]
                lnv = _sb(ph, nc, "lnv", [128, TT], F32)
                rstd = _sb(ph, nc, "rstd", [128, TT], F32)
                rstdq = _sb(ph, nc, "rstdq", [128, TT], F32)
                rstdkv = _sb(ph, nc, "rstdkv", [128, TT], F32)
                cT = _sb(ph, nc, "cT", [128, 5, TT], F32)
                cn = _sb(ph, nc, "cn", [128, 5, TT], BF16)
                cst = _sb(ph, nc, "cst", [128, 2, TT], F32)
                tm1 = [_sb(ph, nc, "tm1_%d" % i, [128, TT], F32) for i in range(2)]
                tm2 = [_sb(ph, nc, "tm2_%d" % i, [128, TT], F32) for i in range(2)]
                kpr = _sb(ph, nc, "kpr", [128, TT], BF16)
                QTm = _sb(ph, nc, "QTm", [128, 8, TT], BF16)
                KTm = _sb(ph, nc, "KTm", [128, 8, TT], BF16)
                Vm = _sb(ph, nc, "Vm", [128, 4, 512], BF16)
                QTs = _sb(ph, nc, "QTs", [128, 4, TT], BF16)
                KTs = _sb(ph, nc, "KTs", [128, 4, TT], BF16)
                Vs = _sb(ph, nc, "Vs", [128, 4, 512], BF16)
                B_xtok = Buf("xtok")
                B_xT = [Buf("xT%d" % i) for i in range(8)]
                B_sqc = [Buf("sqc%d" % i) for i in range(3)]
                B_nT2 = [[Buf("nT%d_%d" % (j, i)) for i in range(8)] for j in range(2)]
                B_lnv, B_rstd, B_rstdq, B_rstdkv = Buf("lnv"), Buf("rstd"), Buf("rstdq"), Buf("rstdkv")
                B_cT = [Buf("cT%d" % i) for i in range(5)]
                B_cn = [Buf("cn%d" % i) for i in range(5)]
                B_cst = Buf("cst")
                B_cst1 = Buf("cst1")
                B_tm1 = [Buf("tm1_%d" % i) for i in range(2)]
                B_tm2 = [Buf("tm2_%d" % i) for i in range(2)]
                B_kpr = Buf("kpr")
                B_QTm = [Buf("QTm%d" % i) for i in range(8)]
                B_QTmr = [Buf("QTmr%d" % i) for i in range(8)]
                B_KTm = [Buf("KTm%d" % i) for i in range(8)]
                B_KTmr = [Buf("KTmr%d" % i) for i in range(8)]
                B_Vm = [Buf("Vm%d" % i) for i in range(4)]
                B_QTs = [Buf("QTs%d" % i) for i in range(4)]
                B_KTs = [Buf("KTs%d" % i) for i in range(4)]
                B_Vs = [Buf("Vs%d" % i) for i in range(4)]
                O_QTm, O_KTm, O_Vm, O_QTs, O_KTs, O_Vs = [Buf("own%d" % i) for i in range(6)]
                bk = Rot(banks)
                sqr = Rot(list(zip(sqc, B_sqc)))
                tmr = Rot(list(zip(tm1, B_tm1, tm2, B_tm2)))
                alt = [0]

                def evac_copy(out, in_, reads, writes):
                    alt[0] += 1
                    if alt[0] % 2:
                        acopy(out, in_, reads, writes)
                    else:
                        vcopy("dve", out, in_, reads, writes)

                def stats(srcs, nfeat, dst, dstb):
                    bt, bb = bk.next()
                    for i, (ap, b) in enumerate(srcs):
                        sq, sqb = sqr.next()
                        act(sq[:, :], ap, AF.Square, [b], [sqb])
                        mm(bt[:, :], onesF[:, :], sq[:, :], i == 0, i == len(srcs) - 1, [sqb, B_const], [bb])
                    act(lnv[:, :], bt[:, :], AF.Ln, [bb], [B_lnv], scale=1.0 / nfeat, bias=EPS)
                    act(dst[:, :], lnv[:, :], AF.Exp, [B_lnv], [dstb], scale=-0.5)

                RR = slice(64, 96)

                def front(t):
                    tsl = slice(t * TT, (t + 1) * TT)
                    nT, B_nT = nT2[t % 2], B_nT2[t % 2]
                    S.dma("sp", xtok[:], x[tsl, :].rearrange("(j p) d -> p j d", p=128), writes=[B_xtok], owner=B_xtok)
                    for c in range(8):
                        bt, bb = bk.next()
                        for j in range(4):
                            tr(bt[:, j * 128:(j + 1) * 128], xtok[:, j, c * 128:(c + 1) * 128], identF[:, :],
                               [B_xtok, B_const], [bb])
                        evac_copy(xT[:, c, :], bt[:, :], [bb], [B_xT[c]])
                    stats([(xT[:, c, :], B_xT[c]) for c in range(8)], float(D), rstd, B_rstd)
                    for c in range(8):
                        stt(nT[:, c, :], xT[:, c, :], gcols[:, GM + c:GM + c + 1], rstd[:, :], ALU.mult, ALU.mult,
                            [B_xT[c], B_gc, B_rstd], [B_nT[c]])
                def mid(t):
                    tsl = slice(t * TT, (t + 1) * TT)
                    nT, B_nT = nT2[t % 2], B_nT2[t % 2]
                    S.dma("sp", cst[64:96, 0, :], cos_d[:, tsl], reads=[B_cos], writes=[B_cst], owner=B_cst)
                    S.dma("sp", cst[64:96, 1, :], sin_d[:, tsl], reads=[B_sin], writes=[B_cst1], owner=B_cst1)

                    def wa_b(col0, M):
                        r = []
                        if col0 < 1152:
                            r += B_WA
                        if col0 + M > 1152:
                            r += B_WAh
                        return r

                    def proj_chunk(col0, M):
                        bt, bb = bk.next()
                        wb = wa_b(col0, M)
                        for kc in range(8):
                            mm(bt[0:M, :], WAs[:, kc, col0:col0 + M], nT[:, kc, :], kc == 0, kc == 7,
                               [B_nT[kc]] + wb, [bb])
                        return bt, bb

                    for i in range(5):
                        bt, bb = proj_chunk(i * 128, 128)
                        evac_copy(cT[:, i, :], bt[:, :], [bb], [B_cT[i]])
                    for i in range(4):
                        bt, bb = proj_chunk(672 + i * 128, 128)
                        evac_copy(QTs[:, i, :], bt[:, :], [bb], [B_QTs[i]])
                    for i in range(4):
                        bt, bb = proj_chunk(1184 + i * 128, 128)
                        evac_copy(KTs[:, i, :], bt[:, :], [bb], [B_KTs[i]])
                    btp, bbp = proj_chunk(576, 96)
                    bts, bbs = proj_chunk(2208, 96)
                    t1, b1, t2, b2 = tmr.next()
                    tt("dve", t1[RR, :], btp[RR, :], cst[RR, 0, :], ALU.mult, [bbp, B_cst], [b1])
                    tt("dve", t2[RR, :], bts[RR, :], cst[RR, 1, :], ALU.mult, [bbs, B_cst1], [b2])
                    tt("pool", kpr[RR, :], t1[RR, :], t2[RR, :], ALU.add, [b1, b2], [B_kpr])
                    for j in range(4):
                        bt, bb = bk.next()
                        for kc in range(8):
                            mm(bt[:, :], nT[:, kc, j * 128:(j + 1) * 128], WAs[:, kc, 1696:2208], kc == 0, kc == 7,
                               [B_nT[kc]] + B_WAh, [bb])
                        evac_copy(Vs[:, j, :], bt[:, :], [bb], [B_Vs[j]])
                def back(t):
                    tsl = slice(t * TT, (t + 1) * TT)
                    stats([(cT[:, i, :], B_cT[i]) for i in range(3)], 384.0, rstdq, B_rstdq)
                    stats([(cT[:, i, :], B_cT[i]) for i in range(3, 5)], 256.0, rstdkv, B_rstdkv)
                    for i in range(5):
                        rs_, rb_ = (rstdq, B_rstdq) if i < 3 else (rstdkv, B_rstdkv)
                        gc = (GQ + i) if i < 3 else (GKV + i - 3)
                        stt(cn[:, i, :], cT[:, i, :], gcols[:, gc:gc + 1], rs_[:, :], ALU.mult, ALU.mult,
                            [B_cT[i], B_gc, rb_], [B_cn[i]])
                    for h in range(8):
                        bta, bba = bk.next()
                        btb, bbb = bk.next()
                        for kc in range(3):
                            mm(bta[0:96, :], wq2s[:, kc, h * 96:(h + 1) * 96], cn[:, kc, :], kc == 0, kc == 2,
                               [B_cn[kc], B_wq2[kc]], [bba])
                        for kc in range(3):
                            mm(btb[0:96, :], wq2s[:, kc, 768 + h * 96:768 + (h + 1) * 96], cn[:, kc, :], kc == 0, kc == 2,
                               [B_cn[kc], B_wq2[kc]], [bbb])
                        acopy(QTm[0:64, h, :], bta[0:64, :], [bba], [B_QTm[h]])
                        t1, b1, t2, b2 = tmr.next()
                        tt("dve", t1[RR, :], bta[RR, :], cst[RR, 0, :], ALU.mult, [bba, B_cst], [b1])
                        tt("dve", t2[RR, :], btb[RR, :], cst[RR, 1, :], ALU.mult, [bbb, B_cst1], [b2])
                        tt("pool", QTm[RR, h, :], t1[RR, :], t2[RR, :], ALU.add, [b1, b2], [B_QTmr[h]])
                    for h in range(8):
                        bt, bb = bk.next()
                        for kc in range(2):
                            mm(bt[0:64, :], wkvs[:, kc, h * 128:h * 128 + 64], cn[:, 3 + kc, :], kc == 0, kc == 1,
                               [B_cn[3 + kc], B_wkv[kc]], [bb])
                        evac_copy(KTm[0:64, h, :], bt[0:64, :], [bb], [B_KTm[h]])
                        vcopy("pool", KTm[RR, h, :], kpr[RR, :], [B_kpr], [B_KTmr[h]])
                    for j in range(4):
                        bt, bb = bk.next()
                        for kc in range(2):
                            mm(bt[:, :].rearrange("p (h c) -> p h c", c=64), cn[:, 3 + kc, j * 128:(j + 1) * 128],
                               wkvs[:, kc, :].rearrange("p (h c) -> p h c", c=128)[:, :, 64:128], kc == 0, kc == 1,
                               [B_cn[3 + kc], B_wkv[kc]], [bb])
                        evac_copy(Vm[:, j, :], bt[:, :], [bb], [B_Vm[j]])
                    S.dma("sp", qtm_d[:, :, tsl].rearrange("h d t -> d h t"), QTm[0:96, :, :],
                          reads=B_QTm + B_QTmr, writes=[B_qtm], owner=O_QTm)
                    S.dma("sp", ktm_d[:, :, tsl].rearrange("h d t -> d h t"), KTm[0:96, :, :],
                          reads=B_KTm + B_KTmr, writes=[B_ktm], owner=O_KTm)
                    S.dma("sp", vm_d[tsl, :].rearrange("(j p) c -> p j c", p=128), Vm[:],
                          reads=B_Vm, writes=[B_vm], owner=O_Vm)
                    S.dma("sp", qts_d[:, tsl].rearrange("(c p) t -> p c t", p=128), QTs[:],
                          reads=B_QTs, writes=[B_qts], owner=O_QTs)
                    S.dma("sp", kts_d[:, tsl].rearrange("(c p) t -> p c t", p=128), KTs[:],
                          reads=B_KTs, writes=[B_kts], owner=O_KTs)
                    S.dma("sp", vs_d[tsl, :].rearrange("(j p) c -> p j c", p=128), Vs[:],
                          reads=B_Vs, writes=[B_vs], owner=O_Vs)

                front(0)
                for t in range(NT):
                    mid(t)
                    if t + 1 < NT:
                        front(t + 1)
                    back(t)
                S.barrier()
        stA.close()

        C.__dict__.update(locals())
        if "B" in phases:
            phase_B(C)
        if "C" in phases:
            phase_C(C)

        if debug:
            Bd_ = Buf("dbgown")
            for nm, src in [("dbg_qtm", qtm_d), ("dbg_ktm", ktm_d), ("dbg_vm", vm_d), ("dbg_qts", qts_d),
                            ("dbg_kts", kts_d), ("dbg_vs", vs_d), ("dbg_oa", oa_d), ("dbg_ob", ob_d),
                            ("dbg_cos", cos_d), ("dbg_sin", sin_d)]:
                S.barrier()
                S.dma("sp", dbg[nm], src, owner=Bd_)
        S.barrier()
    return nc


def phase_B(C):
    nc, S, banks = C.nc, C.S, C.banks
    act, vcopy, acopy, tt, ts, stt, mm = C.act, C.vcopy, C.acopy, C.tt, C.ts, C.stt, C.mm
    onesB, tinclB, B_const, e0B = C.onesB, C.tinclB, C.B_const, C.e0B
    with ExitStack() as phm:
        negB = _sb(phm, nc, "negB", [128, 512], BF16)
        maskM = [_sb(phm, nc, "maskM%d" % j, [128, 512], BF16) for j in range(4)]
        maskS = [_sb(phm, nc, "maskS%d" % j, [128, 512], BF16) for j in range(4)]
        B_mask = Buf("masks")
        S.op("pool", lambda: nc.gpsimd.memset(negB[:], -10000.0), writes=[B_mask])
        for j in range(4):
            S.op("pool", lambda: nc.gpsimd.affine_select(out=maskM[j][:], in_=negB[:], pattern=[[-1, 512]],
                                                          compare_op=ALU.is_ge, fill=0.0, base=j * 128 - 1,
                                                          channel_multiplier=1), reads=[B_mask], writes=[B_mask])
            S.op("pool", lambda: nc.gpsimd.affine_select(out=maskS[j][:], in_=negB[:], pattern=[[-1, 512]],
                                                          compare_op=ALU.is_ge, fill=0.0, base=j * 128,
                                                          channel_multiplier=1), reads=[B_mask], writes=[B_mask])
        phase_B_inner(C, maskM, maskS, B_mask)


def phase_B_inner(C, maskM, maskS, B_mask):
    nc, S, banks = C.nc, C.S, C.banks
    act, vcopy, acopy, tt, ts, stt, mm = C.act, C.vcopy, C.acopy, C.tt, C.ts, C.stt, C.mm
    onesB, tinclB, B_const, e0B = C.onesB, C.tinclB, C.B_const, C.e0B
    with ExitStack() as ph:
        QT = [_sb(ph, nc, "mQT%d" % i, [128, S_LEN], BF16) for i in range(2)]
        KT = [_sb(ph, nc, "mKT%d" % i, [128, S_LEN], BF16) for i in range(2)]
        VV = [_sb(ph, nc, "mV%d" % i, [128, 32, 128], BF16) for i in range(2)]
        B_QT = [Buf("mQT%d" % i) for i in range(2)]
        B_KT = [Buf("mKT%d" % i) for i in range(2)]
        B_VV = [Buf("mV%d" % i) for i in range(2)]
        B_V1 = [Buf("mV1_%d" % i) for i in range(2)]
        Pt = [_sb(ph, nc, "Pt%d" % i, [128, 512], BF16) for i in range(4)]
        B_Pt = [Buf("Pt%d" % i) for i in range(4)]
        rec = [_sb(ph, nc, "rec%d" % i, [128, 512], F32) for i in range(2)]
        B_rec = [Buf("rec%d" % i) for i in range(2)]
        ob = [_sb(ph, nc, "ob%d" % i, [128, 512], BF16) for i in range(2)]
        B_ob_ = [Buf("ob%d" % i) for i in range(2)]
        for i in range(2):
            S.op("pool", lambda: nc.gpsimd.memset(VV[i][:, :, 64:128], 1.0), writes=[B_V1[i]])

        def load_mla(h):
            i = h % 2
            S.dma("sp", QT[i][0:96, :], C.qtm_d[h, :, :], reads=[C.B_qtm], writes=[B_QT[i]], owner=B_QT[i])
            S.dma("sp", KT[i][0:96, :], C.ktm_d[h, :, :], reads=[C.B_ktm], writes=[B_KT[i]], owner=B_KT[i])
            S.dma("sp", VV[i][:, :, 0:64], C.vm_d[:, h * 64:(h + 1) * 64].rearrange("(k p) c -> p k c", p=128),
                  reads=[C.B_vm], writes=[B_VV[i]], owner=B_VV[i])

        Pt2 = [_sb(ph, nc, "Pt2_%d" % i, [128, 1024], BF16) for i in range(3)]
        B_Pt2 = [Buf("Pt2_%d" % i) for i in range(3)]
        spr = Rot([0, 1])
        ptr = Rot(list(zip(Pt2, B_Pt2)))
        fin = Rot(list(zip(rec, B_rec, ob, B_ob_)))
        pd = C.pd
        load_mla(0)
        for h in range(8):
            if h + 1 < 8:
                load_mla(h + 1)
            i = h % 2
            units = [(qi, kb) for qi in range(8) for kb in range(0, 4 * qi + 4, 2)]
            n = len(units)
            st = {}

            def s1(s):
                qi, kb0 = units[s]
                pi = spr.next()
                for sl in range(2):
                    kb = kb0 + sl
                    bt, bb = banks[2 * pi + sl]
                    diag = kb >= 4 * qi
                    mm(bt[:, :], KT[i][0:96, kb * 128:(kb + 1) * 128], QT[i][0:96, qi * 512:(qi + 1) * 512], True, not diag,
                       [B_KT[i], B_QT[i]], [bb])
                    if diag:
                        mm(bt[:, :], C.identB[:, :], maskM[kb - 4 * qi][:, :], False, True, [B_const, B_mask], [bb])
                pt, pb = ptr.next()
                act(pt[:, :], pd[pi][:, :], AF.Exp, [banks[2 * pi][1], banks[2 * pi + 1][1]], [pb], scale=MLA_SCALE)
                st[s] = (pt, pb)

            def s2(s):
                qi, kb0 = units[s]
                pt, pb = st.pop(s)
                ot, obuf = banks[4 + (qi % 2)]
                last = 4 * qi + 3
                for sl in range(2):
                    kb = kb0 + sl
                    mm(ot[:, :], VV[i][:, kb, :], pt[:, sl * 512:(sl + 1) * 512], kb == 0, kb == last,
                       [pb, B_VV[i], B_V1[i]], [obuf])
                if kb0 + 1 == last:
                    rc, rb, o_, o_b = fin.next()
                    S.op("dve", lambda: nc.vector.reciprocal(out=rc[0:64, :], in_=ot[64:128, :]), [obuf], [rb])
                    tt("dve", o_[0:64, :], ot[0:64, :], rc[0:64, :], ALU.mult, [obuf, rb], [o_b])
                    S.dma("sp", C.oa_d[h * 64:(h + 1) * 64, qi * 512:(qi + 1) * 512], o_[0:64, :],
                          reads=[o_b], writes=[C.B_oa], owner=o_b)

            SK = 1
            for step in range(n + SK):
                if step < n:
                    s1(step)
                if step - SK >= 0:
                    s2(step - SK)
        S.barrier()

    with ExitStack() as ph:
        QT = [_sb(ph, nc, "sQT%d" % i, [128, S_LEN], BF16) for i in range(2)]
        KT = [_sb(ph, nc, "sKT%d" % i, [128, S_LEN], BF16) for i in range(2)]
        VV = [_sb(ph, nc, "sV%d" % i, [128, 32, 128], BF16) for i in range(2)]
        B_QT = [Buf("sQT%d" % i) for i in range(2)]
        B_KT = [Buf("sKT%d" % i) for i in range(2)]
        B_VV = [Buf("sV%d" % i) for i in range(2)]
        B_pad = Buf("sPad")
        for i in range(2):
            S.op("pool", lambda: nc.gpsimd.memset(QT[i][64:128, :], 0.0), writes=[B_pad])
            S.op("pool", lambda: nc.gpsimd.memset(KT[i][64:128, :], 0.0), writes=[B_pad])
            S.op("pool", lambda: nc.gpsimd.memset(VV[i][:, :, 64:128], 0.0), writes=[B_pad])
        ee = [_sb(ph, nc, "ee%d" % i, [128, 1024], F32) for i in range(3)]
        B_ee = [Buf("ee%d" % i) for i in range(3)]
        ln = [_sb(ph, nc, "ln%d" % i, [128, 1024], BF16) for i in range(4)]
        B_ln = [Buf("ln%d" % i) for i in range(4)]
        B_lnz = [Buf("lnz%d" % i) for i in range(4)]
        te = [_sb(ph, nc, "te%d" % i, [128, 1024], F32) for i in range(2)]
        B_te = [Buf("te%d" % i) for i in range(2)]
        aa = [_sb(ph, nc, "aa%d" % i, [128, 1024], BF16) for i in range(3)]
        B_aa = [Buf("aa%d" % i) for i in range(3)]
        B_aaz = [Buf("aaz%d" % i) for i in range(3)]
        ob = [_sb(ph, nc, "sob%d" % i, [128, 512], BF16) for i in range(2)]
        B_ob_ = [Buf("sob%d" % i) for i in range(2)]

        def load_sb(h):
            i = h % 2
            S.dma("sp", QT[i][0:64, :], C.qts_d[h * 64:(h + 1) * 64, :], reads=[C.B_qts], writes=[B_QT[i]], owner=B_QT[i])
            S.dma("sp", KT[i][0:64, :], C.kts_d[h * 64:(h + 1) * 64, :], reads=[C.B_kts], writes=[B_KT[i]], owner=B_KT[i])
            S.dma("sp", VV[i][:, :, 0:64], C.vs_d[:, h * 64:(h + 1) * 64].rearrange("(k p) c -> p k c", p=128),
                  reads=[C.B_vs], writes=[B_VV[i]], owner=B_VV[i])

        zpr = Rot([0, 3])
        eer = Rot(list(zip(ee, B_ee)))
        lnr = Rot(list(zip(ln, B_ln, B_lnz)))
        ter = Rot(list(zip(te, B_te)))
        aar = Rot(list(zip(aa, B_aa, B_aaz)))
        obr = Rot(list(zip(ob, B_ob_)))
        pd = C.pd
        load_sb(0)
        for h in range(8):
            if h + 1 < 8:
                load_sb(h + 1)
            i = h % 2
            units = []
            for m in range(4):
                ta = [(2 * m, kb) for kb in range(8 * m + 3, -1, -1)]
                tb_ = [(2 * m + 1, kb) for kb in range(8 * m + 7, -1, -1)]
                for idx in range(len(tb_)):
                    units.append((ta[idx] if idx < len(ta) else None, tb_[idx]))
            n = len(units)
            st = {}

            def tl(u):
                return [(sl, t) for sl, t in enumerate(u) if t is not None]

            def s1(s):
                u = units[s]
                zi = zpr.next()
                for sl, (qi, kb) in tl(u):
                    bt, bb = banks[2 * zi + sl]
                    diag = kb >= 4 * qi
                    mm(bt[:, :], KT[i][:, kb * 128:(kb + 1) * 128], QT[i][:, qi * 512:(qi + 1) * 512], True, not diag,
                       [B_KT[i], B_QT[i], B_pad], [bb])
                    if diag:
                        mm(bt[:, :], C.identB[:, :], maskS[kb - 4 * qi][:, :], False, True, [B_const, B_mask], [bb])
                lo = 0 if u[0] is not None else 512
                qi1, kb1 = u[1]
                c0 = max(0, kb1 - 4 * qi1) * 128
                s0 = lo // 512

                def vw(ap):
                    if c0 == 0:
                        return ap[:, lo:1024]
                    return ap[:, :].rearrange("p (s c) -> p s c", s=2)[:, s0:2, c0:512]

                def zw(ap):
                    return ap[:, :].rearrange("p (s c) -> p s c", s=2)[:, s0:2, 0:c0]
                zb = [banks[2 * zi + sl][1] for sl, _ in tl(u)]
                e_, eb = eer.next()
                act(vw(e_), vw(pd[zi]), AF.Exp, zb, [eb], scale=SB_SCALE)
                l_, lb, lbz = lnr.next()
                if c0 > 0:
                    S.op("pool", lambda: nc.gpsimd.memset(zw(l_), 0.0), writes=[lbz])
                act(vw(l_), vw(e_), AF.Ln, [eb], [lb], bias=1.0)
                st[s] = (e_, eb, l_, [lb, lbz], lo, vw, zw, c0)

            def s2a(s):
                e_, eb, l_, lb, lo = st[s][:5]
                for sl, (qi, kb) in tl(units[s]):
                    ct, cb = banks[2 + sl]
                    mm(ct[:, :], tinclB[:, :], l_[:, sl * 512:(sl + 1) * 512], kb == 4 * qi + 3, kb == 0,
                       lb + [B_const], [cb], sgc=True)

            def s2b(s):
                e_, eb, l_, lb, lo, vw, zw, c0 = st[s]
                cbs = [banks[2 + sl][1] for sl, _ in tl(units[s])]
                t_, tb = ter.next()
                act(vw(t_), vw(pd[1]), AF.Exp, cbs, [tb], scale=-1.0)
                a_, ab, abz = aar.next()
                if c0 > 0:
                    S.op("pool", lambda: nc.gpsimd.memset(zw(a_), 0.0), writes=[abz])
                tt("dve", vw(a_), vw(e_), vw(t_), ALU.mult, [eb, tb], [ab])
                st[s] = (e_, eb, l_, lb, lo, a_, [ab, abz])

            def s2u(s):
                e_, eb, l_, lb, lo, a_, ab = st[s]
                for sl, (qi, kb) in tl(units[s]):
                    if kb > 0:
                        ct, cb = banks[2 + sl]
                        mm(ct[:, :], e0B[:, :], l_[:, sl * 512:(sl + 1) * 512], False, False, lb + [B_const], [cb], sgc=True)

            def s3(s):
                e_, eb, l_, lb, lo, a_, ab = st.pop(s)
                for sl, (qi, kb) in tl(units[s]):
                    ot, obuf = banks[4 + sl]
                    mm(ot[:, :], VV[i][:, kb, :], a_[:, sl * 512:(sl + 1) * 512], kb == 4 * qi + 3, kb == 0,
                       ab + [B_VV[i], B_pad], [obuf])
                    if kb == 0:
                        o_, o_b = obr.next()
                        vcopy("dve", o_[0:64, :], ot[0:64, :], [obuf], [o_b])
                        S.dma("sp", C.ob_d[h * 64:(h + 1) * 64, qi * 512:(qi + 1) * 512], o_[0:64, :],
                              reads=[o_b], writes=[C.B_ob], owner=o_b)

            for step in range(n + 2):
                if step < n:
                    s1(step)
                if step - 2 >= 0:
                    s2u(step - 2)
                if 0 <= step - 1 < n:
                    s2a(step - 1)
                    s2b(step - 1)
                if step - 2 >= 0:
                    s3(step - 2)
        S.barrier()


def phase_C(C):
    nc, S, banks = C.nc, C.S, C.banks
    act, vcopy, acopy, tt, ts, stt, mm, tr = C.act, C.vcopy, C.acopy, C.tt, C.ts, C.stt, C.mm, C.tr
    onesF, identF, B_const, gcols, B_gc = C.onesF, C.identF, C.B_const, C.gcols, C.B_gc
    GM, GF, GP = C.GM, C.GF, C.GP
    wsc = C.wsc
    with ExitStack() as ph:
        xtok = _sb(ph, nc, "c_xtok", [128, 4, D], F32)
        hT2 = [_sb(ph, nc, "c_hT_%d" % i, [128, 8, TT], F32) for i in range(2)]
        nX = _sb(ph, nc, "c_nX", [128, 8, TT], BF16)
        rstd_x = _sb(ph, nc, "c_rstd_x", [128, TT], F32)
        rstd_e = _sb(ph, nc, "c_rstd_e", [128, TT], F32)
        rstd_o = _sb(ph, nc, "c_rstd_o", [128, TT], F32)
        nT = _sb(ph, nc, "c_nT", [128, 8, TT], BF16)
        lnv = _sb(ph, nc, "c_lnv", [128, TT], F32)
        rstd = _sb(ph, nc, "c_rstd", [128, TT], F32)
        oT = _sb(ph, nc, "c_oT", [128, 8, TT], BF16)
        sgab = _sb(ph, nc, "c_sgab", [128, 8, TT], F32)
        sgv = sgab[:, :, :].bitcast(BF16)
        mf = _sb(ph, nc, "c_mf", [128, 8, TT], F32)
        actT = _sb(ph, nc, "c_actT", [128, 22, TT], BF16)
        mg = actT
        ptok = _sb(ph, nc, "c_ptok", [128, 4, 256], F32)
        pT = _sb(ph, nc, "c_pT", [128, 2, TT], BF16)
        tmpa = [_sb(ph, nc, "c_tmpa%d" % i, [128, TT], F32) for i in range(2)]
        tmpb = [_sb(ph, nc, "c_tmpb%d" % i, [128, TT], F32) for i in range(2)]
        wbuf = [_sb(ph, nc, "c_w%d" % i, [128, 8, 512], BF16) for i in range(4)]
        gfin = _sb(ph, nc, "c_gfin", [128, D], F32)
        otok = [_sb(ph, nc, "c_otok%d" % i, [128, D], F32) for i in range(3)]
        ssq = [_sb(ph, nc, "c_ssq%d" % i, [128, TT], F32) for i in range(3)]
        B_ssq = [Buf("c_ssq%d" % i) for i in range(3)]
        ssr = Rot(list(zip(ssq, B_ssq)))
        ssum = _sb(ph, nc, "c_ssum", [128, 4], F32)
        B_xtok = Buf("c_xtok")
        B_hT2 = [[Buf("c_hT%d_%d" % (j, i)) for i in range(8)] for j in range(2)]
        B_nX = [Buf("c_nX%d" % i) for i in range(8)]
        B_rstd_x, B_rstd_e, B_rstd_o = Buf("c_rstd_x"), Buf("c_rstd_e"), Buf("c_rstd_o")
        B_sq8 = [Buf("c_sq8_%d" % i) for i in range(8)]
        B_nT = [Buf("c_nT%d" % i) for i in range(8)]
        B_lnv, B_rstd = Buf("c_lnv"), Buf("c_rstd")
        B_oTa, B_oTb = Buf("c_oTa"), Buf("c_oTb")
        B_sga = [Buf("c_sga%d" % i) for i in range(8)]
        B_sgb = [Buf("c_sgb%d" % i) for i in range(8)]
        B_mf = [Buf("c_mf%d" % i) for i in range(8)]
        B_actT = [Buf("c_actT%d" % i) for i in range(22)]
        B_mg = B_actT[0:8]
        B_ptok = Buf("c_ptok")
        B_pT = [Buf("c_pT%d" % i) for i in range(2)]
        B_tmpa = [Buf("c_tmpa%d" % i) for i in range(2)]
        B_tmpb = [Buf("c_tmpb%d" % i) for i in range(2)]
        B_w = [Buf("c_w%d" % i) for i in range(4)]
        B_gfin = Buf("c_gfin")
        B_otok = [Buf("c_otok%d" % i) for i in range(3)]
        B_ssum = Buf("c_ssum")
        bk = Rot(banks)
        wrot = Rot(list(zip(wbuf, B_w)))
        tar = Rot(list(zip(tmpa, B_tmpa)))
        tbr = Rot(list(zip(tmpb, B_tmpb)))
        otr = Rot(list(zip(otok, B_otok)))
        alt = [0]
        S.dma("sp", gfin[:], C.gfin_d[0:1, :].broadcast_to([128, D]), writes=[B_gfin], owner=B_gfin)

        def evac_copy(out, in_, reads, writes):
            alt[0] += 1
            if alt[0] % 2:
                acopy(out, in_, reads, writes)
            else:
                vcopy("dve", out, in_, reads, writes)

        def stats_pre(srcs):
            sqs = []
            for i, (ap, b) in enumerate(srcs):
                sq, sqb = sgab[:, i, :], [B_sga[i], B_sgb[i]]
                act(sq, ap, AF.Square, [b], sqb)
                sqs.append((sq, sqb))
            lvl = sqs
            k = 0
            while len(lvl) > 2:
                nxt = []
                for j in range(0, len(lvl), 2):
                    (a_, ab_), (b_, bb_) = lvl[j], lvl[j + 1]
                    eng = "pool" if k % 2 == 0 else "dve"
                    k += 1
                    tt(eng, a_, a_, b_, ALU.add, ab_ + bb_, ab_)
                    nxt.append((a_, ab_))
                lvl = nxt
            res, resb = ssr.next()
            (a_, ab_), (b_, bb_) = lvl
            tt("dve", res[:, :], a_, b_, ALU.add, ab_ + bb_, [resb])
            return res, resb

        def stats_post(pre, nfeat, dst, dstb):
            res, resb = pre
            bt, bb = bk.next()
            mm(bt[:, :], onesF[:, :], res[:, :], True, True, [resb, B_const], [bb])
            act(lnv[:, :], bt[:, :], AF.Ln, [bb], [B_lnv], scale=1.0 / nfeat, bias=EPS)
            act(dst[:, :], lnv[:, :], AF.Exp, [B_lnv], [dstb], scale=-0.5)

        def stats(srcs, nfeat, dst, dstb):
            stats_post(stats_pre(srcs), nfeat, dst, dstb)

        def linear(wname, K, col0, ncols_total, rhs_fn, rhs_bufs, evac_fn):
            Wd, _, wdeps = wsc[wname]
            KC = K // 128
            for cg0 in range(0, ncols_total, 512):
                ncol = min(512, ncols_total - cg0)
                noc = ncol // 128
                bks = [bk.next() for _ in range(noc)]
                for kg0 in range(0, KC, 8):
                    nk = min(8, KC - kg0)
                    wt, wb = wrot.next()
                    S.dma("sp", wt[:, 0:nk, 0:ncol],
                          Wd[kg0 * 128:(kg0 + nk) * 128, col0 + cg0:col0 + cg0 + ncol].rearrange("(k p) c -> p k c", p=128),
                          reads=wdeps, writes=[wb], owner=wb)
                    for oc in range(noc):
                        bt, bb = bks[oc]
                        for k in range(nk):
                            kc = kg0 + k
                            mm(bt[:, :], wt[:, k, oc * 128:(oc + 1) * 128], rhs_fn(kc), kc == 0, kc == KC - 1,
                               [wb, rhs_bufs[kc]], [bb])
                for oc in range(noc):
                    evac_fn(cg0 // 128 + oc, bks[oc])

        def front(t):
            tsl = slice(t * TT, (t + 1) * TT)
            hT, B_hT = hT2[t % 2], B_hT2[t % 2]
            S.dma("sp", xtok[:], C.x[tsl, :].rearrange("(j p) d -> p j d", p=128), writes=[B_xtok], owner=B_xtok)
            S.dma("sp", oT[:, 0:4, :], C.oa_d[:, tsl].rearrange("(c p) t -> p c t", p=128), reads=[C.B_oa],
                  writes=[B_oTa], owner=B_oTa)
            S.dma("sp", oT[:, 4:8, :], C.ob_d[:, tsl].rearrange("(c p) t -> p c t", p=128), reads=[C.B_ob],
                  writes=[B_oTb], owner=B_oTb)
            S.dma("sp", ptok[:], C.p_in[tsl, :].rearrange("(j p) d -> p j d", p=128), writes=[B_ptok], owner=B_ptok)
            for c in range(8):
                bt, bb = bk.next()
                for j in range(4):
                    tr(bt[:, j * 128:(j + 1) * 128], xtok[:, j, c * 128:(c + 1) * 128], identF[:, :],
                       [B_xtok, B_const], [bb])
                evac_copy(hT[:, c, :], bt[:, :], [bb], [B_hT[c]])
            return stats_pre([(hT[:, c, :], B_hT[c]) for c in range(8)])

        def front_b(t, pre):
            hT, B_hT = hT2[t % 2], B_hT2[t % 2]
            stats_post(pre, float(D), rstd_x, B_rstd_x)
            for c in range(8):
                stt(nX[:, c, :], hT[:, c, :], gcols[:, GM + c:GM + c + 1], rstd_x[:, :], ALU.mult, ALU.mult,
                    [B_hT[c], B_gc, B_rstd_x], [B_nX[c]])

        def gates(t):
            linear("WG", D, 0, D, lambda kc: nX[:, kc, :], B_nX,
                   lambda oc, b: act(sgv[:, oc, 0:TT], b[0][:, :], AF.Sigmoid, [b[1]], [B_sga[oc]]))
            linear("WG", D, D, D, lambda kc: nX[:, kc, :], B_nX,
                   lambda oc, b: act(sgv[:, oc, TT:2 * TT], b[0][:, :], AF.Sigmoid, [b[1]], [B_sgb[oc]]))

        def mix(t):
            hT, B_hT = hT2[t % 2], B_hT2[t % 2]
            linear("brA", 512, 0, D, lambda kc: oT[:, kc, :], [B_oTa] * 4,
                   lambda oc, b: tt("dve", mf[:, oc, :], b[0][:, :], sgv[:, oc, 0:TT], ALU.mult, [b[1], B_sga[oc]], [B_mf[oc]]))

            def ev_brB(oc, b):
                ta, tab = tar.next()
                tt("dve", ta[:, :], b[0][:, :], sgv[:, oc, TT:2 * TT], ALU.mult, [b[1], B_sgb[oc]], [tab])
                tt("pool", mg[:, oc, :], ta[:, :], mf[:, oc, :], ALU.add, [tab, B_mf[oc]], [B_mg[oc]])
            linear("brB", 512, 0, D, lambda kc: oT[:, 4 + kc, :], [B_oTb] * 4, ev_brB)
            linear("wout", D, 0, D, lambda kc: mg[:, kc, :], B_mg,
                   lambda oc, b: tt("dve", hT[:, oc, :], b[0][:, :], hT[:, oc, :], ALU.add, [b[1], B_hT[oc]], [B_hT[oc]]))
            if C.debug and t == 0:
                S.dma("sp", C.dbg["dbg_h1"].rearrange("(c p) t -> p c t", p=128), hT[:], reads=B_hT, owner=Buf("dh1"))
                S.dma("sp", C.dbg["dbg_mg"].rearrange("(c p) t -> p c t", p=128), mg[:, 0:8, :], reads=B_mg, owner=Buf("dmg"))

        def ple(t):
            for c in range(2):
                bt, bb = bk.next()
                for j in range(4):
                    tr(bt[:, j * 128:(j + 1) * 128], ptok[:, j, c * 128:(c + 1) * 128], identF[:, :],
                       [B_ptok, B_const], [bb])
                evac_copy(pT[:, c, :], bt[:, :], [bb], [B_pT[c]])
            linear("wpp", 256, 0, D, lambda kc: pT[:, kc, :], B_pT,
                   lambda oc, b: acopy(mf[:, oc, :], b[0][:, :], [b[1]], [B_mf[oc]]))
            return stats_pre([(mf[:, c, :], B_mf[c]) for c in range(8)])

        def h_pre(t):
            hT, B_hT = hT2[t % 2], B_hT2[t % 2]
            return stats_pre([(hT[:, c, :], B_hT[c]) for c in range(8)])

        def ffn(t, pre):
            hT, B_hT = hT2[t % 2], B_hT2[t % 2]
            stats_post(pre, float(D), rstd, B_rstd)
            for c in range(8):
                stt(nT[:, c, :], hT[:, c, :], gcols[:, GF + c:GF + c + 1], rstd[:, :], ALU.mult, ALU.mult,
                    [B_hT[c], B_gc, B_rstd], [B_nT[c]])
            linear("wgate", D, 0, DFF, lambda kc: nT[:, kc, :], B_nT,
                   lambda oc, b: act(actT[:, oc, :], b[0][:, :], AF.Silu, [b[1]], [B_actT[oc]]))
            linear("wup", D, 0, DFF, lambda kc: nT[:, kc, :], B_nT,
                   lambda oc, b: tt("dve", actT[:, oc, :], b[0][:, :], actT[:, oc, :], ALU.mult, [b[1], B_actT[oc]], [B_actT[oc]]))
            linear("wdown", DFF, 0, D, lambda kc: actT[:, kc, :], B_actT,
                   lambda oc, b: tt("dve", hT[:, oc, :], b[0][:, :], hT[:, oc, :], ALU.add, [b[1], B_hT[oc]], [B_hT[oc]]))
            if C.debug and t == 0:
                S.dma("sp", C.dbg["dbg_h2"].rearrange("(c p) t -> p c t", p=128), hT[:], reads=B_hT, owner=Buf("dh2"))

        def pgate(t):
            hT, B_hT = hT2[t % 2], B_hT2[t % 2]
            for c in range(8):
                vcopy("pool", nT[:, c, :], hT[:, c, :], [B_hT[c]], [B_nT[c]])

            def ev_pg(oc, b):
                ta, tab = tar.next()
                tb_, tbb = tbr.next()
                act(ta[:, :], b[0][:, :], AF.Sigmoid, [b[1]], [tab])
                stt(tb_[:, :], mf[:, oc, :], gcols[:, GP + oc:GP + oc + 1], rstd_e[:, :], ALU.mult, ALU.mult,
                    [B_mf[oc], B_gc, B_rstd_e], [tbb])
                tt("pool", tb_[:, :], tb_[:, :], ta[:, :], ALU.mult, [tbb, tab], [tbb])
                tt("pool", hT[:, oc, :], hT[:, oc, :], tb_[:, :], ALU.add, [B_hT[oc], tbb], [B_hT[oc]])
            linear("wpg", D, 0, D, lambda kc: nT[:, kc, :], B_nT, ev_pg)
            if C.debug and t == 0:
                S.dma("sp", C.dbg["dbg_h3"].rearrange("(c p) t -> p c t", p=128), hT[:], reads=B_hT, owner=Buf("dh3"))

        def final(t, pre):
            hT, B_hT = hT2[t % 2], B_hT2[t % 2]
            stats_post(pre, float(D), rstd_o, B_rstd_o)
            for c in range(8):
                tt("pool", hT[:, c, :], hT[:, c, :], rstd_o[:, :], ALU.mult, [B_hT[c], B_rstd_o], [B_hT[c]])
            for j in range(4):
                ot_, otb = otr.next()
                for half in range(2):
                    bt, bb = bk.next()
                    for c4 in range(4):
                        c = half * 4 + c4
                        tr(bt[:, c4 * 128:(c4 + 1) * 128], hT[:, c, j * 128:(j + 1) * 128], identF[:, :],
                           [B_hT[c], B_const], [bb])
                    tt("dve", ot_[:, half * 512:(half + 1) * 512], bt[:, :], gfin[:, half * 512:(half + 1) * 512],
                       ALU.mult, [bb, B_gfin], [otb])
                r0 = t * TT + j * 128
                S.dma("sp", C.out_d[r0:r0 + 128, :], ot_[:, :], reads=[otb], writes=[C.B_out], owner=otb)

        front_b(0, front(0))
        gates(0)
        for t in range(NT):
            mix(t)
            p1 = h_pre(t)
            p2 = ple(t)
            p3 = front(t + 1) if t + 1 < NT else None
            ffn_norm_pre = p1
            stats_post(p2, float(D), rstd_e, B_rstd_e)
            ffn(t, ffn_norm_pre)
            if p3 is not None:
                front_b(t + 1, p3)
            pgate(t)
            p4 = h_pre(t)
            if t + 1 < NT:
                gates(t + 1)
            final(t, p4)
        S.barrier()


def prep_inputs(inputs):
    f = lambda k: np.ascontiguousarray(np.asarray(inputs[k]))
    w_in = f("w_in")[0]
    kswap = np.concatenate([w_in[:, 576:640], w_in[:, 656:672], w_in[:, 640:656]], axis=1)
    WA = np.ascontiguousarray(np.concatenate([w_in[:, 0:2208], kswap], axis=1))
    WG = np.ascontiguousarray(w_in[:, 2208:4256])
    wq = f("w_q_b")[0].reshape(384, 8, 96)
    wq_swap = np.concatenate([wq[:, :, 0:64], wq[:, :, 80:96], wq[:, :, 64:80]], axis=2)
    wq2 = np.ascontiguousarray(np.concatenate([wq.reshape(384, 768), wq_swap.reshape(384, 768)], axis=1))
    col = lambda g: g.reshape(-1, 128).T
    gcols = np.ascontiguousarray(np.concatenate(
        [col(f("g_mix")[0]), col(f("g_q_a")[0]), col(f("g_kv_a")[0]), col(f("g_ffn")[0]), col(f("g_ple")[0])],
        axis=1).astype(np.float32))
    consts = np.zeros((128, 2), np.float32)
    inv_freq = (1.0 / (10000.0 ** (np.arange(0, 32, 2, dtype=np.float32) / 32.0))).astype(np.float32)
    for j in range(32):
        consts[64 + j, 0] = inv_freq[j % 16]
        consts[64 + j, 1] = -1.0 if j < 16 else 1.0
    shared = {
        "WA": WA, "WG": WG, "wq2": wq2, "wkv": f("w_kv_b")[0],
        "brA": f("w_br_mla")[0], "brB": f("w_br_sb")[0], "wout": f("w_out")[0],
        "wgate": f("w_ffn_gate")[0], "wup": f("w_ffn_up")[0], "wdown": f("w_ffn_down")[0],
        "wpg": f("w_ple_gate")[0], "wpp": f("w_ple_proj")[0],
        "gcols": gcols, "gfin": f("g_final").reshape(1, D), "consts": consts,
    }
    x = f("x")
    p = f("p")[0]
    pos = f("positions").astype(np.int32)
    maps = []
    for b in range(x.shape[0]):
        m = dict(shared)
        m["x"] = np.ascontiguousarray(x[b])
        m["p"] = np.ascontiguousarray(p[b])
        m["pos"] = np.ascontiguousarray(pos[b].reshape(1, S_LEN))
        maps.append(m)
    return maps


def kernel(**inputs):
    maps = prep_inputs(inputs)
    nc = build_program()
    res = run_bass_kernel_spmd(nc, maps, core_ids=list(range(len(maps))))
    return np.stack([np.asarray(r["out"]) for r in res.results], axis=0).astype(np.float32)
```

```python
import math
from contextlib import ExitStack

import numpy as np
import concourse.bass as bass
import concourse.mybir as mybir
from concourse.bass_utils import run_bass_kernel_spmd

F32 = mybir.dt.float32
BF16 = mybir.dt.bfloat16
I32 = mybir.dt.int32
AF = mybir.ActivationFunctionType
ALU = mybir.AluOpType

S_LEN = 4096
D = 1024
DFF = 2816
TT = 512
NT = S_LEN // TT
EPS = 1e-6
MLA_SCALE = 1.0 / math.sqrt(96.0)
SB_SCALE = 1.0 / 8.0
TWO_PI = 2.0 * math.pi
C1 = 6.28125
C2 = TWO_PI - C1
PI_C = 3.1415925
SAME_ENGINE_SYNC = True


class Buf:
    __slots__ = ("name", "acc", "w", "r", "dsem", "dcnt")

    def __init__(self, name, acc=False):
        self.name = name
        self.acc = acc
        self.w = {}
        self.r = {}
        self.dsem = None
        self.dcnt = 0


class Rot:
    def __init__(self, items):
        self.items = list(items)
        self.i = 0

    def next(self):
        it = self.items[self.i % len(self.items)]
        self.i += 1
        return it


class Sched:
    def __init__(self, nc, es):
        self.nc = nc
        self.es = es
        self.eng = {"pe": nc.tensor, "act": nc.scalar, "dve": nc.vector,
                    "pool": nc.gpsimd, "sp": nc.sync}
        self.esem = {e: es.enter_context(nc.semaphore("s_" + e))
                     for e in ["pe", "act", "dve", "pool"]}
        self.ecnt = {e: 0 for e in self.esem}
        self.waited = {e: {} for e in self.eng}
        self.dbufs = []
        self.nsem = 0

    def _wait(self, eng, recs):
        need = {}
        for (sem, val, e) in recs:
            if e == eng and (eng == "pe" or not SAME_ENGINE_SYNC):
                continue
            k = id(sem)
            if k not in need or val > need[k][1]:
                need[k] = (sem, val)
        for k, (sem, val) in need.items():
            if self.waited[eng].get(k, 0) >= val:
                continue
            self.eng[eng].wait_ge(sem, val)
            self.waited[eng][k] = val

    @staticmethod
    def _deps(reads, writes):
        recs = []
        for b in reads:
            recs += list(b.w.values())
        for b in writes:
            if not b.acc:
                recs += list(b.w.values())
            recs += list(b.r.values())
        return recs

    def op(self, eng, fn, reads=(), writes=()):
        self._wait(eng, self._deps(reads, writes))
        inst = fn()
        self.ecnt[eng] += 1
        inst.then_inc(self.esem[eng], 1)
        rec = (self.esem[eng], self.ecnt[eng], eng)
        for b in reads:
            b.r[eng] = rec
        for b in writes:
            if b.acc:
                b.w[eng] = rec
            else:
                b.w = {eng: rec}
                b.r = {}
        return inst

    def dma(self, q, out, in_, reads=(), writes=(), owner=None):
        self._wait(q, self._deps(reads, writes))
        if owner.dsem is None:
            owner.dsem = self.es.enter_context(self.nc.semaphore("d%d" % self.nsem))
            self.nsem += 1
            self.dbufs.append(owner)
        inst = self.eng[q].dma_start(out=out, in_=in_)
        owner.dcnt += 16
        inst.then_inc(owner.dsem, 16)
        rec = (owner.dsem, owner.dcnt, "dma")
        key = ("dma", id(owner))
        for b in reads:
            b.r[key] = rec
        for b in writes:
            if b.acc:
                b.w[key] = rec
            else:
                b.w = {key: rec}
                b.r = {}
        return inst

    def barrier(self, skip=()):
        sk = set(id(b) for b in skip)
        recs = [(self.esem[e], self.ecnt[e], "x") for e in self.esem if self.ecnt[e] > 0]
        recs += [(b.dsem, b.dcnt, "dma") for b in self.dbufs if b.dcnt > 0 and id(b) not in sk]
        for e in self.eng:
            self._wait(e, recs)


class Ctx:
    pass


def _sb(stack, nc, name, shape, dt):
    return stack.enter_context(nc.sbuf_tensor("sb_" + name, shape, dt))


def build_program(debug=False, phases="0ABC"):
    nc = bass.Bass("TRN2", target_bir_lowering=False)
    C = Ctx()
    C.nc = nc

    def din(name, shape, dt=F32):
        return nc.dram_tensor(name, shape, dt, kind="ExternalInput").ap()

    def dscr(name, shape, dt):
        return nc.dram_tensor(name, shape, dt, kind="Internal").ap()

    x = din("x", [S_LEN, D])
    p_in = din("p", [S_LEN, 256])
    pos = din("pos", [1, S_LEN], I32)
    WA = din("WA", [D, 2304])
    WG = din("WG", [D, 2048])
    wq2 = din("wq2", [384, 1536])
    wkv = din("wkv", [256, 1024])
    brA = din("brA", [512, D])
    brB = din("brB", [512, D])
    wout = din("wout", [D, D])
    wgate = din("wgate", [D, DFF])
    wup = din("wup", [D, DFF])
    wdown = din("wdown", [DFF, D])
    wpg = din("wpg", [D, D])
    wpp = din("wpp", [256, D])
    gcols_d = din("gcols", [128, 29])
    gfin_d = din("gfin", [1, D])
    consts_d = din("consts", [128, 2])
    out_d = nc.dram_tensor("out", [S_LEN, D], F32, kind="ExternalOutput").ap()

    cos_d = dscr("cos_d", [32, S_LEN], F32)
    sin_d = dscr("sin_d", [32, S_LEN], F32)
    qtm_d = dscr("qtm_d", [8, 96, S_LEN], BF16)
    ktm_d = dscr("ktm_d", [8, 96, S_LEN], BF16)
    vm_d = dscr("vm_d", [S_LEN, 512], BF16)
    qts_d = dscr("qts_d", [512, S_LEN], BF16)
    kts_d = dscr("kts_d", [512, S_LEN], BF16)
    vs_d = dscr("vs_d", [S_LEN, 512], BF16)
    oa_d = dscr("oa_d", [512, S_LEN], BF16)
    ob_d = dscr("ob_d", [512, S_LEN], BF16)
    wsc = {}
    for nm, src in [("WG", WG), ("brA", brA), ("brB", brB), ("wout", wout), ("wgate", wgate),
                    ("wup", wup), ("wdown", wdown), ("wpg", wpg), ("wpp", wpp)]:
        wsc[nm] = (dscr(nm + "_bf", list(src.shape), BF16), src, [])
    dbg = {}
    if debug:
        for nm, shp, dt in [("dbg_qtm", [8, 96, S_LEN], BF16), ("dbg_ktm", [8, 96, S_LEN], BF16),
                            ("dbg_vm", [S_LEN, 512], BF16), ("dbg_qts", [512, S_LEN], BF16),
                            ("dbg_kts", [512, S_LEN], BF16), ("dbg_vs", [S_LEN, 512], BF16),
                            ("dbg_oa", [512, S_LEN], BF16), ("dbg_ob", [512, S_LEN], BF16),
                            ("dbg_cos", [32, S_LEN], F32), ("dbg_sin", [32, S_LEN], F32),
                            ("dbg_h1", [D, TT], F32), ("dbg_h2", [D, TT], F32), ("dbg_h3", [D, TT], F32),
                            ("dbg_mg", [D, TT], BF16)]:
            dbg[nm] = nc.dram_tensor(nm, shp, dt, kind="ExternalOutput").ap()

    B_cos = Buf("cos_d")
    B_sin = Buf("sin_d")
    B_qtm = Buf("qtm_d", acc=True)
    B_ktm = Buf("ktm_d", acc=True)
    B_vm = Buf("vm_d", acc=True)
    B_qts = Buf("qts_d", acc=True)
    B_kts = Buf("kts_d", acc=True)
    B_vs = Buf("vs_d", acc=True)
    B_oa = Buf("oa_d", acc=True)
    B_ob = Buf("ob_d", acc=True)
    B_out = Buf("out_d", acc=True)

    with ExitStack() as es:
        S = Sched(nc, es)
        C.S = S
        pd = [es.enter_context(nc.psum_tensor("pd%d" % i, [128, 1024], F32)) for i in range(4)]
        banks = [(pd[i // 2][:, (i % 2) * 512:(i % 2 + 1) * 512], Buf("pb%d" % i)) for i in range(8)]

        def act(out, in_, func, reads, writes, scale=None, bias=None, accum_out=None):
            kw = {}
            if scale is not None:
                kw["scale"] = scale
            if bias is not None:
                kw["bias"] = bias
            if accum_out is not None:
                kw["accum_out"] = accum_out
            return S.op("act", lambda: nc.scalar.activation(out=out, in_=in_, func=func, **kw), reads, writes)

        def vcopy(eng, out, in_, reads, writes):
            e = {"dve": nc.vector, "pool": nc.gpsimd}[eng]
            return S.op(eng, lambda: e.tensor_copy(out=out, in_=in_), reads, writes)

        def acopy(out, in_, reads, writes):
            return S.op("act", lambda: nc.scalar.copy(out=out, in_=in_), reads, writes)

        def tt(eng, out, in0, in1, op, reads, writes):
            e = {"dve": nc.vector, "pool": nc.gpsimd}[eng]
            return S.op(eng, lambda: e.tensor_tensor(out=out, in0=in0, in1=in1, op=op), reads, writes)

        def ts(eng, out, in0, s1, op0, reads, writes, s2=None, op1=None):
            e = {"dve": nc.vector, "pool": nc.gpsimd}[eng]
            if op1 is None:
                return S.op(eng, lambda: e.tensor_scalar(out=out, in0=in0, scalar1=s1, scalar2=None, op0=op0), reads, writes)
            return S.op(eng, lambda: e.tensor_scalar(out=out, in0=in0, scalar1=s1, scalar2=s2, op0=op0, op1=op1), reads, writes)

        def stt(out, in0, scalar, in1, op0, op1, reads, writes):
            return S.op("dve", lambda: nc.vector.scalar_tensor_tensor(out=out, in0=in0, scalar=scalar, in1=in1, op0=op0, op1=op1), reads, writes)

        def mm(out, lhsT, rhs, start, stop, reads, writes, sgc=False):
            return S.op("pe", lambda: nc.tensor.matmul(out, lhsT=lhsT, rhs=rhs, start=start, stop=stop,
                                                       skip_group_check=sgc), reads, writes)

        def tr(out, in_, ident, reads, writes):
            return S.op("pe", lambda: nc.tensor.transpose(out=out, in_=in_, identity=ident), reads, writes)

        identF = _sb(es, nc, "identF", [128, 128], F32)
        onesF = _sb(es, nc, "onesF", [128, 128], F32)
        onesB = _sb(es, nc, "onesB", [128, 128], BF16)
        tinclB = _sb(es, nc, "tinclB", [128, 128], BF16)
        e0B = _sb(es, nc, "e0B", [128, 128], BF16)
        gcols = _sb(es, nc, "gcols", [128, 29], F32)
        consts = _sb(es, nc, "consts", [128, 2], F32)
        B_const = Buf("const")
        B_gc = Buf("gcols")
        B_cs = Buf("consts")
        S.dma("sp", gcols[:], gcols_d[:, :], writes=[B_gc], owner=B_gc)
        S.dma("sp", consts[:], consts_d[:, :], writes=[B_cs], owner=B_cs)
        S.op("pool", lambda: nc.gpsimd.memset(onesF[:], 1.0), writes=[B_const])
        S.op("pool", lambda: nc.gpsimd.memset(onesB[:], 1.0), writes=[B_const])
        S.op("pool", lambda: nc.gpsimd.affine_select(out=identF[:], in_=onesF[:], pattern=[[1, 128]],
                                                      compare_op=ALU.is_equal, fill=0.0, base=0,
                                                      channel_multiplier=-1), reads=[B_const], writes=[B_const])
        S.op("pool", lambda: nc.gpsimd.affine_select(out=tinclB[:], in_=onesB[:], pattern=[[-1, 128]],
                                                      compare_op=ALU.is_ge, fill=0.0, base=0,
                                                      channel_multiplier=1), reads=[B_const], writes=[B_const])
        S.op("pool", lambda: nc.gpsimd.affine_select(out=e0B[:], in_=onesB[:], pattern=[[1, 128]],
                                                      compare_op=ALU.is_ge, fill=0.0, base=-1,
                                                      channel_multiplier=-1), reads=[B_const], writes=[B_const])
        identB = _sb(es, nc, "identB", [128, 128], BF16)
        S.op("pool", lambda: nc.gpsimd.tensor_copy(out=identB[:], in_=identF[:]), reads=[B_const], writes=[B_const])
        GM, GQ, GKV, GF, GP = 0, 8, 11, 13, 21

        stA = ExitStack()
        WAs = _sb(stA, nc, "WAs", [128, 8, 2304], BF16)
        wq2s = _sb(stA, nc, "wq2s", [128, 3, 1536], BF16)
        wkvs = _sb(stA, nc, "wkvs", [128, 2, 1024], BF16)
        B_WA = [Buf("WA%d" % k) for k in range(8)]
        B_WAh = [Buf("WAh%d" % k) for k in range(8)]
        B_wq2 = [Buf("wq2_%d" % k) for k in range(3)]
        B_wkv = [Buf("wkv_%d" % k) for k in range(2)]
        if "A" in phases:
            for k in range(8):
                S.dma("pool", WAs[:, k, 0:1152], WA[k * 128:(k + 1) * 128, 0:1152], writes=[B_WA[k]], owner=B_WA[k])
                S.dma("pool", WAs[:, k, 1152:2304], WA[k * 128:(k + 1) * 128, 1152:2304], writes=[B_WAh[k]], owner=B_WAh[k])
            for k in range(3):
                S.dma("pool", wq2s[:, k, :], wq2[k * 128:(k + 1) * 128, :], writes=[B_wq2[k]], owner=B_wq2[k])
            for k in range(2):
                S.dma("pool", wkvs[:, k, :], wkv[k * 128:(k + 1) * 128, :], writes=[B_wkv[k]], owner=B_wkv[k])
        if "C" in phases:
            for nm in ["WG", "brA", "brB", "wout", "wgate", "wup", "wdown", "wpg", "wpp"]:
                dst, src, bl = wsc[nm]
                K_, M_ = src.shape
                ncs = (M_ + 2047) // 2048
                cw = M_ // ncs
                for r0 in range(0, K_, 1024):
                    r1 = min(K_, r0 + 1024)
                    for ci in range(ncs):
                        b = Buf("%s_%d_%d" % (nm, r0, ci))
                        S.dma("pool", dst[r0:r1, ci * cw:(ci + 1) * cw], src[r0:r1, ci * cw:(ci + 1) * cw],
                              writes=[b], owner=b)
                        bl.append(b)

        if "0" in phases:
            with ExitStack() as ph:
                posi = _sb(ph, nc, "posi", [128, S_LEN], I32)
                t_a = _sb(ph, nc, "t_a", [128, S_LEN], F32)
                t_b = _sb(ph, nc, "t_b", [128, S_LEN], F32)
                t_c = _sb(ph, nc, "t_c", [128, S_LEN], F32)
                t_d = _sb(ph, nc, "t_d", [128, S_LEN], F32)
                t_e = _sb(ph, nc, "t_e", [128, S_LEN], F32)
                t_f = _sb(ph, nc, "t_f", [128, S_LEN], F32)
                Bp, Ba, Bb, Bc, Bd, Be, Bf = [Buf("ph0_%d" % i) for i in range(7)]
                R = slice(64, 96)
                S.dma("sp", posi[R, :], pos[0:1, :].broadcast_to([32, S_LEN]), writes=[Bp], owner=Bp)
                vcopy("dve", t_a[R, :], posi[R, :], [Bp], [Ba])
                ts("dve", t_b[R, :], t_a[R, :], consts[R, 0:1], ALU.mult, [Ba, B_cs], [Bb])
                ts("dve", t_a[R, :], t_b[R, :], 1.0 / TWO_PI, ALU.mult, [Bb], [Ba])
                vcopy("dve", posi[R, :], t_a[R, :], [Ba], [Bp])
                vcopy("dve", t_a[R, :], posi[R, :], [Bp], [Ba])
                stt(t_c[R, :], t_a[R, :], -C1, t_b[R, :], ALU.mult, ALU.add, [Ba, Bb], [Bc])
                stt(t_c[R, :], t_a[R, :], -C2, t_c[R, :], ALU.mult, ALU.add, [Ba, Bc], [Bc])
                ts("dve", t_d[R, :], t_c[R, :], -PI_C, ALU.max, [Bc], [Bd], s2=PI_C, op1=ALU.min)
                act(t_e[R, :], t_d[R, :], AF.Sin, [Bd], [Be])
                ts("dve", t_e[R, :], t_e[R, :], consts[R, 1:2], ALU.mult, [Be, B_cs], [Be])
                ts("dve", t_d[R, :], t_c[R, :], math.pi / 2, ALU.add, [Bc], [Bd])
                ts("dve", t_a[R, :], t_d[R, :], math.pi, ALU.is_gt, [Bd], [Ba])
                stt(t_d[R, :], t_a[R, :], -TWO_PI, t_d[R, :], ALU.mult, ALU.add, [Ba, Bd], [Bd])
                ts("dve", t_d[R, :], t_d[R, :], -PI_C, ALU.max, [Bd], [Bd], s2=PI_C, op1=ALU.min)
                act(t_f[R, :], t_d[R, :], AF.Sin, [Bd], [Bf])
                S.dma("sp", cos_d[:, :], t_f[R, :], reads=[Bf], writes=[B_cos], owner=Bf)
                S.dma("sp", sin_d[:, :], t_e[R, :], reads=[Be], writes=[B_sin], owner=Be)
                bg = B_WA + B_WAh + B_wq2 + B_wkv
                for nm_ in wsc:
                    bg = bg + wsc[nm_][2]
                S.barrier(skip=bg)

        if "A" in phases:
            with ExitStack() as ph:
                xtok = _sb(ph, nc, "xtok", [128, 4, D], F32)
                xT = _sb(ph, nc, "xT", [128, 8, TT], F32)
                sq8a = _sb(ph, nc, "sq8a", [128, 8, TT], F32)
                nT2 = [_sb(ph, nc, "nT_%d" % i, [128, 8, TT], BF16) for i in range(2)]
                lnv = _sb(ph, nc, "lnv", [128, TT], F32)
                rstd = _sb(ph, nc, "rstd", [128, TT], F32)
                rstdq = _sb(ph, nc, "rstdq", [128, TT], F32)
                rstdkv = _sb(ph, nc, "rstdkv", [128, TT], F32)
                cT = _sb(ph, nc, "cT", [128, 5, TT], F32)
                cn = _sb(ph, nc, "cn", [128, 5, TT], BF16)
                cst = _sb(ph, nc, "cst", [128, 2, TT], F32)
                tm1 = [_sb(ph, nc, "tm1_%d" % i, [128, TT], F32) for i in range(2)]
                tm2 = [_sb(ph, nc, "tm2_%d" % i, [128, TT], F32) for i in range(2)]
                kpr = _sb(ph, nc, "kpr", [128, TT], BF16)
                QTm = _sb(ph, nc, "QTm", [128, 8, TT], BF16)
                KTm = _sb(ph, nc, "KTm", [128, 8, TT], BF16)
                Vm = _sb(ph, nc, "Vm", [128, 4, 512], BF16)
                QTs = _sb(ph, nc, "QTs", [128, 4, TT], BF16)
                KTs = _sb(ph, nc, "KTs", [128, 4, TT], BF16)
                Vs = _sb(ph, nc, "Vs", [128, 4, 512], BF16)
                B_xtok = Buf("xtok")
                B_xT = [Buf("xT%d" % i) for i in range(8)]
                B_sq8a = [Buf("sq8a%d" % i) for i in range(8)]
                B_nT2 = [[Buf("nT%d_%d" % (j, i)) for i in range(8)] for j in range(2)]
                B_lnv, B_rstd, B_rstdq, B_rstdkv = Buf("lnv"), Buf("rstd"), Buf("rstdq"), Buf("rstdkv")
                B_cT = [Buf("cT%d" % i) for i in range(5)]
                B_cn = [Buf("cn%d" % i) for i in range(5)]
                B_cst = Buf("cst")
                B_cst1 = Buf("cst1")
                B_tm1 = [Buf("tm1_%d" % i) for i in range(2)]
                B_tm2 = [Buf("tm2_%d" % i) for i in range(2)]
                B_kpr = Buf("kpr")
                B_QTm = [Buf("QTm%d" % i) for i in range(8)]
                B_QTmr = [Buf("QTmr%d" % i) for i in range(8)]
                B_KTm = [Buf("KTm%d" % i) for i in range(8)]
                B_KTmr = [Buf("KTmr%d" % i) for i in range(8)]
                B_Vm = [Buf("Vm%d" % i) for i in range(4)]
                B_QTs = [Buf("QTs%d" % i) for i in range(4)]
                B_KTs = [Buf("KTs%d" % i) for i in range(4)]
                B_Vs = [Buf("Vs%d" % i) for i in range(4)]
                O_QTm, O_KTm, O_Vm, O_QTs, O_KTs, O_Vs = [Buf("own%d" % i) for i in range(6)]
                bk = Rot(banks)
                tmr = Rot(list(zip(tm1, B_tm1, tm2, B_tm2)))
                alt = [0]

                def evac_copy(out, in_, reads, writes):
                    alt[0] += 1
                    if alt[0] % 2:
                        acopy(out, in_, reads, writes)
                    else:
                        vcopy("dve", out, in_, reads, writes)

                def stats(srcs, nfeat, dst, dstb):
                    bt, bb = bk.next()
                    lvl = []
                    for i, (ap, b) in enumerate(srcs):
                        sq, sqb = sq8a[:, i, :], B_sq8a[i]
                        act(sq, ap, AF.Square, [b], [sqb])
                        lvl.append((sq, sqb))
                    k = 0
                    while len(lvl) > 2:
                        nxt = []
                        for j in range(0, len(lvl) - 1, 2):
                            (a_, ab_), (b_, bb_) = lvl[j], lvl[j + 1]
                            eng = "pool" if k % 2 == 0 else "dve"
                            k += 1
                            tt(eng, a_, a_, b_, ALU.add, [ab_, bb_], [ab_])
                            nxt.append((a_, ab_))
                        if len(lvl) % 2:
                            nxt.append(lvl[-1])
                        lvl = nxt
                    for i, (sq, sqb) in enumerate(lvl):
                        mm(bt[:, :], onesF[:, :], sq, i == 0, i == len(lvl) - 1, [sqb, B_const], [bb])
                    act(lnv[:, :], bt[:, :], AF.Ln, [bb], [B_lnv], scale=1.0 / nfeat, bias=EPS)
                    act(dst[:, :], lnv[:, :], AF.Exp, [B_lnv], [dstb], scale=-0.5)

                RR = slice(64, 96)

                def front(t):
                    tsl = slice(t * TT, (t + 1) * TT)
                    nT, B_nT = nT2[t % 2], B_nT2[t % 2]
                    S.dma("sp", xtok[:], x[tsl, :].rearrange("(j p) d -> p j d", p=128), writes=[B_xtok], owner=B_xtok)
                    for c in range(8):
                        bt, bb = bk.next()
                        for j in range(4):
                            tr(bt[:, j * 128:(j + 1) * 128], xtok[:, j, c * 128:(c + 1) * 128], identF[:, :],
                               [B_xtok, B_const], [bb])
                        evac_copy(xT[:, c, :], bt[:, :], [bb], [B_xT[c]])
                    stats([(xT[:, c, :], B_xT[c]) for c in range(8)], float(D), rstd, B_rstd)
                    for c in range(8):
                        stt(nT[:, c, :], xT[:, c, :], gcols[:, GM + c:GM + c + 1], rstd[:, :], ALU.mult, ALU.mult,
                            [B_xT[c], B_gc, B_rstd], [B_nT[c]])
                def mid(t):
                    tsl = slice(t * TT, (t + 1) * TT)
                    nT, B_nT = nT2[t % 2], B_nT2[t % 2]
                    S.dma("sp", cst[64:96, 0, :], cos_d[:, tsl], reads=[B_cos], writes=[B_cst], owner=B_cst)
                    S.dma("sp", cst[64:96, 1, :], sin_d[:, tsl], reads=[B_sin], writes=[B_cst1], owner=B_cst1)

                    def wa_b(col0, M):
                        r = []
                        if col0 < 1152:
                            r += B_WA
                        if col0 + M > 1152:
                            r += B_WAh
                        return r

                    def proj_chunk(col0, M):
                        bt, bb = bk.next()
                        wb = wa_b(col0, M)
                        for kc in range(8):
                            mm(bt[0:M, :], WAs[:, kc, col0:col0 + M], nT[:, kc, :], kc == 0, kc == 7,
                               [B_nT[kc]] + wb, [bb])
                        return bt, bb

                    for i in range(5):
                        bt, bb = proj_chunk(i * 128, 128)
                        evac_copy(cT[:, i, :], bt[:, :], [bb], [B_cT[i]])
                    for i in range(4):
                        bt, bb = proj_chunk(672 + i * 128, 128)
                        evac_copy(QTs[:, i, :], bt[:, :], [bb], [B_QTs[i]])
                    for i in range(4):
                        bt, bb = proj_chunk(1184 + i * 128, 128)
                        evac_copy(KTs[:, i, :], bt[:, :], [bb], [B_KTs[i]])
                    btp, bbp = proj_chunk(576, 96)
                    bts, bbs = proj_chunk(2208, 96)
                    t1, b1, t2, b2 = tmr.next()
                    tt("dve", t1[RR, :], btp[RR, :], cst[RR, 0, :], ALU.mult, [bbp, B_cst], [b1])
                    tt("dve", t2[RR, :], bts[RR, :], cst[RR, 1, :], ALU.mult, [bbs, B_cst1], [b2])
                    tt("pool", kpr[RR, :], t1[RR, :], t2[RR, :], ALU.add, [b1, b2], [B_kpr])
                    for j in range(4):
                        bt, bb = bk.next()
                        for kc in range(8):
                            mm(bt[:, :], nT[:, kc, j * 128:(j + 1) * 128], WAs[:, kc, 1696:2208], kc == 0, kc == 7,
                               [B_nT[kc]] + B_WAh, [bb])
                        evac_copy(Vs[:, j, :], bt[:, :], [bb], [B_Vs[j]])
                def back(t):
                    tsl = slice(t * TT, (t + 1) * TT)
                    stats([(cT[:, i, :], B_cT[i]) for i in range(3)], 384.0, rstdq, B_rstdq)
                    stats([(cT[:, i, :], B_cT[i]) for i in range(3, 5)], 256.0, rstdkv, B_rstdkv)
                    for i in range(5):
                        rs_, rb_ = (rstdq, B_rstdq) if i < 3 else (rstdkv, B_rstdkv)
                        gc = (GQ + i) if i < 3 else (GKV + i - 3)
                        stt(cn[:, i, :], cT[:, i, :], gcols[:, gc:gc + 1], rs_[:, :], ALU.mult, ALU.mult,
                            [B_cT[i], B_gc, rb_], [B_cn[i]])
                    for h in range(8):
                        bta, bba = bk.next()
                        btb, bbb = bk.next()
                        for kc in range(3):
                            mm(bta[0:96, :], wq2s[:, kc, h * 96:(h + 1) * 96], cn[:, kc, :], kc == 0, kc == 2,
                               [B_cn[kc], B_wq2[kc]], [bba])
                        for kc in range(3):
                            mm(btb[0:96, :], wq2s[:, kc, 768 + h * 96:768 + (h + 1) * 96], cn[:, kc, :], kc == 0, kc == 2,
                               [B_cn[kc], B_wq2[kc]], [bbb])
                        acopy(QTm[0:64, h, :], bta[0:64, :], [bba], [B_QTm[h]])
                        t1, b1, t2, b2 = tmr.next()
                        tt("dve", t1[RR, :], bta[RR, :], cst[RR, 0, :], ALU.mult, [bba, B_cst], [b1])
                        tt("dve", t2[RR, :], btb[RR, :], cst[RR, 1, :], ALU.mult, [bbb, B_cst1], [b2])
                        tt("pool", QTm[RR, h, :], t1[RR, :], t2[RR, :], ALU.add, [b1, b2], [B_QTmr[h]])
                    for h in range(8):
                        bt, bb = bk.next()
                        for kc in range(2):
                            mm(bt[0:64, :], wkvs[:, kc, h * 128:h * 128 + 64], cn[:, 3 + kc, :], kc == 0, kc == 1,
                               [B_cn[3 + kc], B_wkv[kc]], [bb])
                        evac_copy(KTm[0:64, h, :], bt[0:64, :], [bb], [B_KTm[h]])
                        vcopy("pool", KTm[RR, h, :], kpr[RR, :], [B_kpr], [B_KTmr[h]])
                    for j in range(4):
                        bt, bb = bk.next()
                        for kc in range(2):
                            mm(bt[:, :].rearrange("p (h c) -> p h c", c=64), cn[:, 3 + kc, j * 128:(j + 1) * 128],
                               wkvs[:, kc, :].rearrange("p (h c) -> p h c", c=128)[:, :, 64:128], kc == 0, kc == 1,
                               [B_cn[3 + kc], B_wkv[kc]], [bb])
                        evac_copy(Vm[:, j, :], bt[:, :], [bb], [B_Vm[j]])
                    S.dma("sp", qtm_d[:, :, tsl].rearrange("h d t -> d h t"), QTm[0:96, :, :],
                          reads=B_QTm + B_QTmr, writes=[B_qtm], owner=O_QTm)
                    S.dma("sp", ktm_d[:, :, tsl].rearrange("h d t -> d h t"), KTm[0:96, :, :],
                          reads=B_KTm + B_KTmr, writes=[B_ktm], owner=O_KTm)
                    S.dma("sp", vm_d[tsl, :].rearrange("(j p) c -> p j c", p=128), Vm[:],
                          reads=B_Vm, writes=[B_vm], owner=O_Vm)
                    S.dma("sp", qts_d[:, tsl].rearrange("(c p) t -> p c t", p=128), QTs[:],
                          reads=B_QTs, writes=[B_qts], owner=O_QTs)
                    S.dma("sp", kts_d[:, tsl].rearrange("(c p) t -> p c t", p=128), KTs[:],
                          reads=B_KTs, writes=[B_kts], owner=O_KTs)
                    S.dma("sp", vs_d[tsl, :].rearrange("(j p) c -> p j c", p=128), Vs[:],
                          reads=B_Vs, writes=[B_vs], owner=O_Vs)

                front(0)
                for t in range(NT):
                    mid(t)
                    if t + 1 < NT:
                        front(t + 1)
                    back(t)
                S.barrier()
        stA.close()

        C.__dict__.update(locals())
        if "B" in phases:
            phase_B(C)
        if "C" in phases:
            phase_C(C)

        if debug:
            Bd_ = Buf("dbgown")
            for nm, src in [("dbg_qtm", qtm_d), ("dbg_ktm", ktm_d), ("dbg_vm", vm_d), ("dbg_qts", qts_d),
                            ("dbg_kts", kts_d), ("dbg_vs", vs_d), ("dbg_oa", oa_d), ("dbg_ob", ob_d),
                            ("dbg_cos", cos_d), ("dbg_sin", sin_d)]:
                S.barrier()
                S.dma("sp", dbg[nm], src, owner=Bd_)
        S.barrier()
    return nc


def phase_B(C):
    nc, S, banks = C.nc, C.S, C.banks
    act, vcopy, acopy, tt, ts, stt, mm = C.act, C.vcopy, C.acopy, C.tt, C.ts, C.stt, C.mm
    onesB, tinclB, B_const, e0B = C.onesB, C.tinclB, C.B_const, C.e0B
    with ExitStack() as phm:
        negB = _sb(phm, nc, "negB", [128, 512], BF16)
        maskM = [_sb(phm, nc, "maskM%d" % j, [128, 512], BF16) for j in range(4)]
        maskS = [_sb(phm, nc, "maskS%d" % j, [128, 512], BF16) for j in range(4)]
        B_mask = Buf("masks")
        S.op("pool", lambda: nc.gpsimd.memset(negB[:], -10000.0), writes=[B_mask])
        for j in range(4):
            S.op("pool", lambda: nc.gpsimd.affine_select(out=maskM[j][:], in_=negB[:], pattern=[[-1, 512]],
                                                          compare_op=ALU.is_ge, fill=0.0, base=j * 128 - 1,
                                                          channel_multiplier=1), reads=[B_mask], writes=[B_mask])
            S.op("pool", lambda: nc.gpsimd.affine_select(out=maskS[j][:], in_=negB[:], pattern=[[-1, 512]],
                                                          compare_op=ALU.is_ge, fill=0.0, base=j * 128,
                                                          channel_multiplier=1), reads=[B_mask], writes=[B_mask])
        phase_B_inner(C, maskM, maskS, B_mask)


def phase_B_inner(C, maskM, maskS, B_mask):
    nc, S, banks = C.nc, C.S, C.banks
    act, vcopy, acopy, tt, ts, stt, mm = C.act, C.vcopy, C.acopy, C.tt, C.ts, C.stt, C.mm
    onesB, tinclB, B_const, e0B = C.onesB, C.tinclB, C.B_const, C.e0B
    with ExitStack() as ph:
        QT = [_sb(ph, nc, "mQT%d" % i, [128, S_LEN], BF16) for i in range(2)]
        KT = [_sb(ph, nc, "mKT%d" % i, [128, S_LEN], BF16) for i in range(2)]
        VV = [_sb(ph, nc, "mV%d" % i, [128, 32, 128], BF16) for i in range(2)]
        B_QT = [Buf("mQT%d" % i) for i in range(2)]
        B_KT = [Buf("mKT%d" % i) for i in range(2)]
        B_VV = [Buf("mV%d" % i) for i in range(2)]
        B_V1 = [Buf("mV1_%d" % i) for i in range(2)]
        Pt = [_sb(ph, nc, "Pt%d" % i, [128, 512], BF16) for i in range(4)]
        B_Pt = [Buf("Pt%d" % i) for i in range(4)]
        rec = [_sb(ph, nc, "rec%d" % i, [128, 512], F32) for i in range(2)]
        B_rec = [Buf("rec%d" % i) for i in range(2)]
        ob = [_sb(ph, nc, "ob%d" % i, [128, 512], BF16) for i in range(2)]
        B_ob_ = [Buf("ob%d" % i) for i in range(2)]
        for i in range(2):
            S.op("pool", lambda: nc.gpsimd.memset(VV[i][:, :, 64:128], 1.0), writes=[B_V1[i]])

        def load_mla(h):
            i = h % 2
            S.dma("sp", QT[i][0:96, :], C.qtm_d[h, :, :], reads=[C.B_qtm], writes=[B_QT[i]], owner=B_QT[i])
            S.dma("sp", KT[i][0:96, :], C.ktm_d[h, :, :], reads=[C.B_ktm], writes=[B_KT[i]], owner=B_KT[i])
            S.dma("sp", VV[i][:, :, 0:64], C.vm_d[:, h * 64:(h + 1) * 64].rearrange("(k p) c -> p k c", p=128),
                  reads=[C.B_vm], writes=[B_VV[i]], owner=B_VV[i])

        Pt2 = [_sb(ph, nc, "Pt2_%d" % i, [128, 1024], BF16) for i in range(3)]
        B_Pt2 = [Buf("Pt2_%d" % i) for i in range(3)]
        spr = Rot([0, 1])
        ptr = Rot(list(zip(Pt2, B_Pt2)))
        fin = Rot(list(zip(rec, B_rec, ob, B_ob_)))
        pd = C.pd
        load_mla(0)
        for h in range(8):
            if h + 1 < 8:
                load_mla(h + 1)
            i = h % 2
            units = [(qi, kb) for qi in range(8) for kb in range(0, 4 * qi + 4, 2)]
            n = len(units)
            st = {}

            def s1(s):
                qi, kb0 = units[s]
                pi = spr.next()
                for sl in range(2):
                    kb = kb0 + sl
                    bt, bb = banks[2 * pi + sl]
                    diag = kb >= 4 * qi
                    mm(bt[:, :], KT[i][0:96, kb * 128:(kb + 1) * 128], QT[i][0:96, qi * 512:(qi + 1) * 512], True, not diag,
                       [B_KT[i], B_QT[i]], [bb])
                    if diag:
                        mm(bt[:, :], C.identB[:, :], maskM[kb - 4 * qi][:, :], False, True, [B_const, B_mask], [bb])
                pt, pb = ptr.next()
                act(pt[:, :], pd[pi][:, :], AF.Exp, [banks[2 * pi][1], banks[2 * pi + 1][1]], [pb], scale=MLA_SCALE)
                st[s] = (pt, pb)

            def s2(s):
                qi, kb0 = units[s]
                pt, pb = st.pop(s)
                ot, obuf = banks[4 + (qi % 2)]
                last = 4 * qi + 3
                for sl in range(2):
                    kb = kb0 + sl
                    mm(ot[:, :], VV[i][:, kb, :], pt[:, sl * 512:(sl + 1) * 512], kb == 0, kb == last,
                       [pb, B_VV[i], B_V1[i]], [obuf])
                if kb0 + 1 == last:
                    rc, rb, o_, o_b = fin.next()
                    S.op("dve", lambda: nc.vector.reciprocal(out=rc[0:64, :], in_=ot[64:128, :]), [obuf], [rb])
                    tt("dve", o_[0:64, :], ot[0:64, :], rc[0:64, :], ALU.mult, [obuf, rb], [o_b])
                    S.dma("sp", C.oa_d[h * 64:(h + 1) * 64, qi * 512:(qi + 1) * 512], o_[0:64, :],
                          reads=[o_b], writes=[C.B_oa], owner=o_b)

            SK = 1
            for step in range(n + SK):
                if step < n:
                    s1(step)
                if step - SK >= 0:
                    s2(step - SK)
        S.barrier()

    with ExitStack() as ph:
        QT = [_sb(ph, nc, "sQT%d" % i, [128, S_LEN], BF16) for i in range(2)]
        KT = [_sb(ph, nc, "sKT%d" % i, [128, S_LEN], BF16) for i in range(2)]
        VV = [_sb(ph, nc, "sV%d" % i, [128, 32, 128], BF16) for i in range(2)]
        B_QT = [Buf("sQT%d" % i) for i in range(2)]
        B_KT = [Buf("sKT%d" % i) for i in range(2)]
        B_VV = [Buf("sV%d" % i) for i in range(2)]
        B_pad = Buf("sPad")
        for i in range(2):
            S.op("pool", lambda: nc.gpsimd.memset(QT[i][64:128, :], 0.0), writes=[B_pad])
            S.op("pool", lambda: nc.gpsimd.memset(KT[i][64:128, :], 0.0), writes=[B_pad])
            S.op("pool", lambda: nc.gpsimd.memset(VV[i][:, :, 64:128], 0.0), writes=[B_pad])
        ee = [_sb(ph, nc, "ee%d" % i, [128, 1024], F32) for i in range(3)]
        B_ee = [Buf("ee%d" % i) for i in range(3)]
        ln = [_sb(ph, nc, "ln%d" % i, [128, 1024], BF16) for i in range(4)]
        B_ln = [Buf("ln%d" % i) for i in range(4)]
        B_lnz = [Buf("lnz%d" % i) for i in range(4)]
        te = [_sb(ph, nc, "te%d" % i, [128, 1024], F32) for i in range(2)]
        B_te = [Buf("te%d" % i) for i in range(2)]
        aa = [_sb(ph, nc, "aa%d" % i, [128, 1024], BF16) for i in range(3)]
        B_aa = [Buf("aa%d" % i) for i in range(3)]
        B_aaz = [Buf("aaz%d" % i) for i in range(3)]
        ob = [_sb(ph, nc, "sob%d" % i, [128, 512], BF16) for i in range(2)]
        B_ob_ = [Buf("sob%d" % i) for i in range(2)]

        def load_sb(h):
            i = h % 2
            S.dma("sp", QT[i][0:64, :], C.qts_d[h * 64:(h + 1) * 64, :], reads=[C.B_qts], writes=[B_QT[i]], owner=B_QT[i])
            S.dma("sp", KT[i][0:64, :], C.kts_d[h * 64:(h + 1) * 64, :], reads=[C.B_kts], writes=[B_KT[i]], owner=B_KT[i])
            S.dma("sp", VV[i][:, :, 0:64], C.vs_d[:, h * 64:(h + 1) * 64].rearrange("(k p) c -> p k c", p=128),
                  reads=[C.B_vs], writes=[B_VV[i]], owner=B_VV[i])

        zpr = Rot([0, 3])
        eer = Rot(list(zip(ee, B_ee)))
        lnr = Rot(list(zip(ln, B_ln, B_lnz)))
        ter = Rot(list(zip(te, B_te)))
        aar = Rot(list(zip(aa, B_aa, B_aaz)))
        obr = Rot(list(zip(ob, B_ob_)))
        pd = C.pd
        load_sb(0)
        for h in range(8):
            if h + 1 < 8:
                load_sb(h + 1)
            i = h % 2
            units = []
            for m in range(4):
                ta = [(2 * m, kb) for kb in range(8 * m + 3, -1, -1)]
                tb_ = [(2 * m + 1, kb) for kb in range(8 * m + 7, -1, -1)]
                for idx in range(len(tb_)):
                    units.append((ta[idx] if idx < len(ta) else None, tb_[idx]))
            n = len(units)
            st = {}

            def tl(u):
                return [(sl, t) for sl, t in enumerate(u) if t is not None]

            def s1(s):
                u = units[s]
                zi = zpr.next()
                for sl, (qi, kb) in tl(u):
                    bt, bb = banks[2 * zi + sl]
                    diag = kb >= 4 * qi
                    mm(bt[:, :], KT[i][:, kb * 128:(kb + 1) * 128], QT[i][:, qi * 512:(qi + 1) * 512], True, not diag,
                       [B_KT[i], B_QT[i], B_pad], [bb])
                    if diag:
                        mm(bt[:, :], C.identB[:, :], maskS[kb - 4 * qi][:, :], False, True, [B_const, B_mask], [bb])
                lo = 0 if u[0] is not None else 512
                qi1, kb1 = u[1]
                c0 = max(0, kb1 - 4 * qi1) * 128
                s0 = lo // 512

                def vw(ap):
                    if c0 == 0:
                        return ap[:, lo:1024]
                    return ap[:, :].rearrange("p (s c) -> p s c", s=2)[:, s0:2, c0:512]

                def zw(ap):
                    return ap[:, :].rearrange("p (s c) -> p s c", s=2)[:, s0:2, 0:c0]
                zb = [banks[2 * zi + sl][1] for sl, _ in tl(u)]
                e_, eb = eer.next()
                act(vw(e_), vw(pd[zi]), AF.Exp, zb, [eb], scale=SB_SCALE)
                l_, lb, lbz = lnr.next()
                if c0 > 0:
                    S.op("pool", lambda: nc.gpsimd.memset(zw(l_), 0.0), writes=[lbz])
                act(vw(l_), vw(e_), AF.Ln, [eb], [lb], bias=1.0)
                st[s] = (e_, eb, l_, [lb, lbz], lo, vw, zw, c0)

            def s2a(s):
                e_, eb, l_, lb, lo = st[s][:5]
                for sl, (qi, kb) in tl(units[s]):
                    ct, cb = banks[2 + sl]
                    mm(ct[:, :], tinclB[:, :], l_[:, sl * 512:(sl + 1) * 512], kb == 4 * qi + 3, kb == 0,
                       lb + [B_const], [cb], sgc=True)

            def s2b(s):
                e_, eb, l_, lb, lo, vw, zw, c0 = st[s]
                cbs = [banks[2 + sl][1] for sl, _ in tl(units[s])]
                t_, tb = ter.next()
                act(vw(t_), vw(pd[1]), AF.Exp, cbs, [tb], scale=-1.0)
                a_, ab, abz = aar.next()
                if c0 > 0:
                    S.op("pool", lambda: nc.gpsimd.memset(zw(a_), 0.0), writes=[abz])
                tt("dve", vw(a_), vw(e_), vw(t_), ALU.mult, [eb, tb], [ab])
                st[s] = (e_, eb, l_, lb, lo, a_, [ab, abz])

            def s2u(s):
                e_, eb, l_, lb, lo, a_, ab = st[s]
                for sl, (qi, kb) in tl(units[s]):
                    if kb > 0:
                        ct, cb = banks[2 + sl]
                        mm(ct[:, :], e0B[:, :], l_[:, sl * 512:(sl + 1) * 512], False, False, lb + [B_const], [cb], sgc=True)

            def s3(s):
                e_, eb, l_, lb, lo, a_, ab = st.pop(s)
                for sl, (qi, kb) in tl(units[s]):
                    ot, obuf = banks[4 + sl]
                    mm(ot[:, :], VV[i][:, kb, :], a_[:, sl * 512:(sl + 1) * 512], kb == 4 * qi + 3, kb == 0,
                       ab + [B_VV[i], B_pad], [obuf])
                    if kb == 0:
                        o_, o_b = obr.next()
                        vcopy("dve", o_[0:64, :], ot[0:64, :], [obuf], [o_b])
                        S.dma("sp", C.ob_d[h * 64:(h + 1) * 64, qi * 512:(qi + 1) * 512], o_[0:64, :],
                              reads=[o_b], writes=[C.B_ob], owner=o_b)

            for step in range(n + 2):
                if step < n:
                    s1(step)
                if step - 2 >= 0:
                    s2u(step - 2)
                if 0 <= step - 1 < n:
                    s2a(step - 1)
                    s2b(step - 1)
                if step - 2 >= 0:
                    s3(step - 2)
        S.barrier()


def phase_C(C):
    nc, S, banks = C.nc, C.S, C.banks
    act, vcopy, acopy, tt, ts, stt, mm, tr = C.act, C.vcopy, C.acopy, C.tt, C.ts, C.stt, C.mm, C.tr
    onesF, identF, B_const, gcols, B_gc = C.onesF, C.identF, C.B_const, C.gcols, C.B_gc
    GM, GF, GP = C.GM, C.GF, C.GP
    wsc = C.wsc
    with ExitStack() as ph:
        xtok = _sb(ph, nc, "c_xtok", [128, 4, D], F32)
        hT2 = [_sb(ph, nc, "c_hT_%d" % i, [128, 8, TT], F32) for i in range(2)]
        nX = _sb(ph, nc, "c_nX", [128, 8, TT], BF16)
        rstd_x = _sb(ph, nc, "c_rstd_x", [128, TT], F32)
        rstd_e = _sb(ph, nc, "c_rstd_e", [128, TT], F32)
        rstd_o = _sb(ph, nc, "c_rstd_o", [128, TT], F32)
        nT = _sb(ph, nc, "c_nT", [128, 8, TT], BF16)
        lnv = _sb(ph, nc, "c_lnv", [128, TT], F32)
        rstd = _sb(ph, nc, "c_rstd", [128, TT], F32)
        oT = _sb(ph, nc, "c_oT", [128, 8, TT], BF16)
        sgab = _sb(ph, nc, "c_sgab", [128, 8, TT], F32)
        sgv = sgab[:, :, :].bitcast(BF16)
        mf = _sb(ph, nc, "c_mf", [128, 8, TT], F32)
        actT = _sb(ph, nc, "c_actT", [128, 22, TT], BF16)
        mg = actT
        ptok = _sb(ph, nc, "c_ptok", [128, 4, 256], F32)
        pT = _sb(ph, nc, "c_pT", [128, 2, TT], BF16)
        tmpa = [_sb(ph, nc, "c_tmpa%d" % i, [128, TT], F32) for i in range(2)]
        tmpb = [_sb(ph, nc, "c_tmpb%d" % i, [128, TT], F32) for i in range(2)]
        wbuf = [_sb(ph, nc, "c_w%d" % i, [128, 8, 512], BF16) for i in range(4)]
        gfin = _sb(ph, nc, "c_gfin", [128, D], F32)
        otok = [_sb(ph, nc, "c_otok%d" % i, [128, D], F32) for i in range(3)]
        ssq = [_sb(ph, nc, "c_ssq%d" % i, [128, TT], F32) for i in range(3)]
        B_ssq = [Buf("c_ssq%d" % i) for i in range(3)]
        ssr = Rot(list(zip(ssq, B_ssq)))
        ssum = _sb(ph, nc, "c_ssum", [128, 4], F32)
        B_xtok = Buf("c_xtok")
        B_hT2 = [[Buf("c_hT%d_%d" % (j, i)) for i in range(8)] for j in range(2)]
        B_nX = [Buf("c_nX%d" % i) for i in range(8)]
        B_rstd_x, B_rstd_e, B_rstd_o = Buf("c_rstd_x"), Buf("c_rstd_e"), Buf("c_rstd_o")
        B_sq8 = [Buf("c_sq8_%d" % i) for i in range(8)]
        B_nT = [Buf("c_nT%d" % i) for i in range(8)]
        B_lnv, B_rstd = Buf("c_lnv"), Buf("c_rstd")
        B_oTa, B_oTb = Buf("c_oTa"), Buf("c_oTb")
        B_sga = [Buf("c_sga%d" % i) for i in range(8)]
        B_sgb = [Buf("c_sgb%d" % i) for i in range(8)]
        B_mf = [Buf("c_mf%d" % i) for i in range(8)]
        B_actT = [Buf("c_actT%d" % i) for i in range(22)]
        B_mg = B_actT[0:8]
        B_ptok = Buf("c_ptok")
        B_pT = [Buf("c_pT%d" % i) for i in range(2)]
        B_tmpa = [Buf("c_tmpa%d" % i) for i in range(2)]
        B_tmpb = [Buf("c_tmpb%d" % i) for i in range(2)]
        B_w = [Buf("c_w%d" % i) for i in range(4)]
        B_gfin = Buf("c_gfin")
        B_otok = [Buf("c_otok%d" % i) for i in range(3)]
        B_ssum = Buf("c_ssum")
        bk = Rot(banks)
        wrot = Rot(list(zip(wbuf, B_w)))
        tar = Rot(list(zip(tmpa, B_tmpa)))
        tbr = Rot(list(zip(tmpb, B_tmpb)))
        otr = Rot(list(zip(otok, B_otok)))
        alt = [0]
        S.dma("sp", gfin[:], C.gfin_d[0:1, :].broadcast_to([128, D]), writes=[B_gfin], owner=B_gfin)

        def evac_copy(out, in_, reads, writes):
            alt[0] += 1
            if alt[0] % 2:
                acopy(out, in_, reads, writes)
            else:
                vcopy("dve", out, in_, reads, writes)

        def stats_pre(srcs):
            sqs = []
            for i, (ap, b) in enumerate(srcs):
                sq, sqb = sgab[:, i, :], [B_sga[i], B_sgb[i]]
                act(sq, ap, AF.Square, [b], sqb)
                sqs.append((sq, sqb))
            lvl = sqs
            k = 0
            while len(lvl) > 2:
                nxt = []
                for j in range(0, len(lvl), 2):
                    (a_, ab_), (b_, bb_) = lvl[j], lvl[j + 1]
                    eng = "pool" if k % 2 == 0 else "dve"
                    k += 1
                    tt(eng, a_, a_, b_, ALU.add, ab_ + bb_, ab_)
                    nxt.append((a_, ab_))
                lvl = nxt
            res, resb = ssr.next()
            (a_, ab_), (b_, bb_) = lvl
            tt("dve", res[:, :], a_, b_, ALU.add, ab_ + bb_, [resb])
            return res, resb

        def stats_post(pre, nfeat, dst, dstb):
            res, resb = pre
            bt, bb = bk.next()
            mm(bt[:, :], onesF[:, :], res[:, :], True, True, [resb, B_const], [bb])
            act(lnv[:, :], bt[:, :], AF.Ln, [bb], [B_lnv], scale=1.0 / nfeat, bias=EPS)
            act(dst[:, :], lnv[:, :], AF.Exp, [B_lnv], [dstb], scale=-0.5)

        def stats(srcs, nfeat, dst, dstb):
            stats_post(stats_pre(srcs), nfeat, dst, dstb)

        def linear(wname, K, col0, ncols_total, rhs_fn, rhs_bufs, evac_fn):
            Wd, _, wdeps = wsc[wname]
            KC = K // 128
            for cg0 in range(0, ncols_total, 512):
                ncol = min(512, ncols_total - cg0)
                noc = ncol // 128
                bks = [bk.next() for _ in range(noc)]
                for kg0 in range(0, KC, 8):
                    nk = min(8, KC - kg0)
                    wt, wb = wrot.next()
                    S.dma("sp", wt[:, 0:nk, 0:ncol],
                          Wd[kg0 * 128:(kg0 + nk) * 128, col0 + cg0:col0 + cg0 + ncol].rearrange("(k p) c -> p k c", p=128),
                          reads=wdeps, writes=[wb], owner=wb)
                    for oc in range(noc):
                        bt, bb = bks[oc]
                        for k in range(nk):
                            kc = kg0 + k
                            mm(bt[:, :], wt[:, k, oc * 128:(oc + 1) * 128], rhs_fn(kc), kc == 0, kc == KC - 1,
                               [wb, rhs_bufs[kc]], [bb])
                for oc in range(noc):
                    evac_fn(cg0 // 128 + oc, bks[oc])

        def front(t):
            tsl = slice(t * TT, (t + 1) * TT)
            hT, B_hT = hT2[t % 2], B_hT2[t % 2]
            S.dma("sp", xtok[:], C.x[tsl, :].rearrange("(j p) d -> p j d", p=128), writes=[B_xtok], owner=B_xtok)
            S.dma("sp", oT[:, 0:4, :], C.oa_d[:, tsl].rearrange("(c p) t -> p c t", p=128), reads=[C.B_oa],
                  writes=[B_oTa], owner=B_oTa)
            S.dma("sp", oT[:, 4:8, :], C.ob_d[:, tsl].rearrange("(c p) t -> p c t", p=128), reads=[C.B_ob],
                  writes=[B_oTb], owner=B_oTb)
            S.dma("sp", ptok[:], C.p_in[tsl, :].rearrange("(j p) d -> p j d", p=128), writes=[B_ptok], owner=B_ptok)
            for c in range(8):
                bt, bb = bk.next()
                for j in range(4):
                    tr(bt[:, j * 128:(j + 1) * 128], xtok[:, j, c * 128:(c + 1) * 128], identF[:, :],
                       [B_xtok, B_const], [bb])
                evac_copy(hT[:, c, :], bt[:, :], [bb], [B_hT[c]])
            return stats_pre([(hT[:, c, :], B_hT[c]) for c in range(8)])

        def front_b(t, pre):
            hT, B_hT = hT2[t % 2], B_hT2[t % 2]
            stats_post(pre, float(D), rstd_x, B_rstd_x)
            for c in range(8):
                stt(nX[:, c, :], hT[:, c, :], gcols[:, GM + c:GM + c + 1], rstd_x[:, :], ALU.mult, ALU.mult,
                    [B_hT[c], B_gc, B_rstd_x], [B_nX[c]])

        def gates(t):
            linear("WG", D, 0, D, lambda kc: nX[:, kc, :], B_nX,
                   lambda oc, b: act(sgv[:, oc, 0:TT], b[0][:, :], AF.Sigmoid, [b[1]], [B_sga[oc]]))
            linear("WG", D, D, D, lambda kc: nX[:, kc, :], B_nX,
                   lambda oc, b: act(sgv[:, oc, TT:2 * TT], b[0][:, :], AF.Sigmoid, [b[1]], [B_sgb[oc]]))

        def mix(t):
            hT, B_hT = hT2[t % 2], B_hT2[t % 2]
            linear("brA", 512, 0, D, lambda kc: oT[:, kc, :], [B_oTa] * 4,
                   lambda oc, b: tt("dve", mf[:, oc, :], b[0][:, :], sgv[:, oc, 0:TT], ALU.mult, [b[1], B_sga[oc]], [B_mf[oc]]))

            def ev_brB(oc, b):
                ta, tab = tar.next()
                tt("dve", ta[:, :], b[0][:, :], sgv[:, oc, TT:2 * TT], ALU.mult, [b[1], B_sgb[oc]], [tab])
                tt("pool", mg[:, oc, :], ta[:, :], mf[:, oc, :], ALU.add, [tab, B_mf[oc]], [B_mg[oc]])
            linear("brB", 512, 0, D, lambda kc: oT[:, 4 + kc, :], [B_oTb] * 4, ev_brB)
            linear("wout", D, 0, D, lambda kc: mg[:, kc, :], B_mg,
                   lambda oc, b: tt("dve", hT[:, oc, :], b[0][:, :], hT[:, oc, :], ALU.add, [b[1], B_hT[oc]], [B_hT[oc]]))
            if C.debug and t == 0:
                S.dma("sp", C.dbg["dbg_h1"].rearrange("(c p) t -> p c t", p=128), hT[:], reads=B_hT, owner=Buf("dh1"))
                S.dma("sp", C.dbg["dbg_mg"].rearrange("(c p) t -> p c t", p=128), mg[:, 0:8, :], reads=B_mg, owner=Buf("dmg"))

        def ple(t):
            for c in range(2):
                bt, bb = bk.next()
                for j in range(4):
                    tr(bt[:, j * 128:(j + 1) * 128], ptok[:, j, c * 128:(c + 1) * 128], identF[:, :],
                       [B_ptok, B_const], [bb])
                evac_copy(pT[:, c, :], bt[:, :], [bb], [B_pT[c]])
            linear("wpp", 256, 0, D, lambda kc: pT[:, kc, :], B_pT,
                   lambda oc, b: acopy(mf[:, oc, :], b[0][:, :], [b[1]], [B_mf[oc]]))
            return stats_pre([(mf[:, c, :], B_mf[c]) for c in range(8)])

        def h_pre(t):
            hT, B_hT = hT2[t % 2], B_hT2[t % 2]
            return stats_pre([(hT[:, c, :], B_hT[c]) for c in range(8)])

        def ffn(t, pre):
            hT, B_hT = hT2[t % 2], B_hT2[t % 2]
            stats_post(pre, float(D), rstd, B_rstd)
            for c in range(8):
                stt(nT[:, c, :], hT[:, c, :], gcols[:, GF + c:GF + c + 1], rstd[:, :], ALU.mult, ALU.mult,
                    [B_hT[c], B_gc, B_rstd], [B_nT[c]])
            linear("wgate", D, 0, DFF, lambda kc: nT[:, kc, :], B_nT,
                   lambda oc, b: act(actT[:, oc, :], b[0][:, :], AF.Silu, [b[1]], [B_actT[oc]]))
            linear("wup", D, 0, DFF, lambda kc: nT[:, kc, :], B_nT,
                   lambda oc, b: tt("dve", actT[:, oc, :], b[0][:, :], actT[:, oc, :], ALU.mult, [b[1], B_actT[oc]], [B_actT[oc]]))
            linear("wdown", DFF, 0, D, lambda kc: actT[:, kc, :], B_actT,
                   lambda oc, b: tt("dve", hT[:, oc, :], b[0][:, :], hT[:, oc, :], ALU.add, [b[1], B_hT[oc]], [B_hT[oc]]))
            if C.debug and t == 0:
                S.dma("sp", C.dbg["dbg_h2"].rearrange("(c p) t -> p c t", p=128), hT[:], reads=B_hT, owner=Buf("dh2"))

        def pgate(t):
            hT, B_hT = hT2[t % 2], B_hT2[t % 2]
            for c in range(8):
                vcopy("pool", nT[:, c, :], hT[:, c, :], [B_hT[c]], [B_nT[c]])

            def ev_pg(oc, b):
                ta, tab = tar.next()
                tb_, tbb = tbr.next()
                act(ta[:, :], b[0][:, :], AF.Sigmoid, [b[1]], [tab])
                stt(tb_[:, :], mf[:, oc, :], gcols[:, GP + oc:GP + oc + 1], rstd_e[:, :], ALU.mult, ALU.mult,
                    [B_mf[oc], B_gc, B_rstd_e], [tbb])
                tt("pool", tb_[:, :], tb_[:, :], ta[:, :], ALU.mult, [tbb, tab], [tbb])
                tt("pool", hT[:, oc, :], hT[:, oc, :], tb_[:, :], ALU.add, [B_hT[oc], tbb], [B_hT[oc]])
            linear("wpg", D, 0, D, lambda kc: nT[:, kc, :], B_nT, ev_pg)
            if C.debug and t == 0:
                S.dma("sp", C.dbg["dbg_h3"].rearrange("(c p) t -> p c t", p=128), hT[:], reads=B_hT, owner=Buf("dh3"))

        def final(t, pre):
            hT, B_hT = hT2[t % 2], B_hT2[t % 2]
            stats_post(pre, float(D), rstd_o, B_rstd_o)
            for c in range(8):
                tt("pool", hT[:, c, :], hT[:, c, :], rstd_o[:, :], ALU.mult, [B_hT[c], B_rstd_o], [B_hT[c]])
            for j in range(4):
                ot_, otb = otr.next()
                for half in range(2):
                    bt, bb = bk.next()
                    for c4 in range(4):
                        c = half * 4 + c4
                        tr(bt[:, c4 * 128:(c4 + 1) * 128], hT[:, c, j * 128:(j + 1) * 128], identF[:, :],
                           [B_hT[c], B_const], [bb])
                    tt("dve", ot_[:, half * 512:(half + 1) * 512], bt[:, :], gfin[:, half * 512:(half + 1) * 512],
                       ALU.mult, [bb, B_gfin], [otb])
                r0 = t * TT + j * 128
                S.dma("sp", C.out_d[r0:r0 + 128, :], ot_[:, :], reads=[otb], writes=[C.B_out], owner=otb)

        front_b(0, front(0))
        gates(0)
        for t in range(NT):
            mix(t)
            p1 = h_pre(t)
            p2 = ple(t)
            p3 = front(t + 1) if t + 1 < NT else None
            ffn_norm_pre = p1
            stats_post(p2, float(D), rstd_e, B_rstd_e)
            ffn(t, ffn_norm_pre)
            if p3 is not None:
                front_b(t + 1, p3)
            pgate(t)
            p4 = h_pre(t)
            if t + 1 < NT:
                gates(t + 1)
            final(t, p4)
        S.barrier()


def prep_inputs(inputs):
    f = lambda k: np.ascontiguousarray(np.asarray(inputs[k]))
    w_in = f("w_in")[0]
    kswap = np.concatenate([w_in[:, 576:640], w_in[:, 656:672], w_in[:, 640:656]], axis=1)
    WA = np.ascontiguousarray(np.concatenate([w_in[:, 0:2208], kswap], axis=1))
    WG = np.ascontiguousarray(w_in[:, 2208:4256])
    wq = f("w_q_b")[0].reshape(384, 8, 96)
    wq_swap = np.concatenate([wq[:, :, 0:64], wq[:, :, 80:96], wq[:, :, 64:80]], axis=2)
    wq2 = np.ascontiguousarray(np.concatenate([wq.reshape(384, 768), wq_swap.reshape(384, 768)], axis=1))
    col = lambda g: g.reshape(-1, 128).T
    gcols = np.ascontiguousarray(np.concatenate(
        [col(f("g_mix")[0]), col(f("g_q_a")[0]), col(f("g_kv_a")[0]), col(f("g_ffn")[0]), col(f("g_ple")[0])],
        axis=1).astype(np.float32))
    consts = np.zeros((128, 2), np.float32)
    inv_freq = (1.0 / (10000.0 ** (np.arange(0, 32, 2, dtype=np.float32) / 32.0))).astype(np.float32)
    for j in range(32):
        consts[64 + j, 0] = inv_freq[j % 16]
        consts[64 + j, 1] = -1.0 if j < 16 else 1.0
    shared = {
        "WA": WA, "WG": WG, "wq2": wq2, "wkv": f("w_kv_b")[0],
        "brA": f("w_br_mla")[0], "brB": f("w_br_sb")[0], "wout": f("w_out")[0],
        "wgate": f("w_ffn_gate")[0], "wup": f("w_ffn_up")[0], "wdown": f("w_ffn_down")[0],
        "wpg": f("w_ple_gate")[0], "wpp": f("w_ple_proj")[0],
        "gcols": gcols, "gfin": f("g_final").reshape(1, D), "consts": consts,
    }
    x = f("x")
    p = f("p")[0]
    pos = f("positions").astype(np.int32)
    maps = []
    for b in range(x.shape[0]):
        m = dict(shared)
        m["x"] = np.ascontiguousarray(x[b])
        m["p"] = np.ascontiguousarray(p[b])
        m["pos"] = np.ascontiguousarray(pos[b].reshape(1, S_LEN))
        maps.append(m)
    return maps


def kernel(**inputs):
    maps = prep_inputs(inputs)
    nc = build_program()
    res = run_bass_kernel_spmd(nc, maps, core_ids=list(range(len(maps))))
    return np.stack([np.asarray(r["out"]) for r in res.results], axis=0).astype(np.float32)
```
